# Optimizing a Trainium2 kernel written in Bass

```python
import jax, jax.numpy as jnp
from jax import lax
import numpy as np

D_MODEL = 1024
BATCH = 8
SEQ = 2048
DEPTH = 1
DEC_BATCH = 128
DEC_SEQ = 8
PAST_LEN = 16384
PAGE_SIZE = 128

N_MEM = 256
RW_HEAD = 64
RW_WIDTH = D_MODEL
RW_HEADS = RW_WIDTH // RW_HEAD
DECAY_LORA = 64
AAA_LORA = 64
RW_SHIFT = 3 * RW_WIDTH + DECAY_LORA + AAA_LORA
LRU_WIDTH = D_MODEL
LRU_BLOCKS = 16
LRU_BLOCK = LRU_WIDTH // LRU_BLOCKS
CONV_W = 4
LRU_C = 8.0
XA_HEADS = 4
XA_WIDTH = D_MODEL
XA_HEAD = XA_WIDTH // XA_HEADS
N_BRANCH = 3
IN_COLS = RW_SHIFT + RW_WIDTH + 2 * LRU_WIDTH + 2 * XA_WIDTH + N_BRANCH * D_MODEL
NORM_EPS = 1e-6
GN_EPS = 64e-5
L2_EPS = 1e-12

kernel_name = "rwkv7_rglru_xattn_gated_hybrid_step"


def rmsnorm(x, g):
    xf = x.astype(jnp.float32)
    y = xf * lax.rsqrt(jnp.mean(xf * xf, axis=-1, keepdims=True) + NORM_EPS)
    return (y * g.astype(jnp.float32)).astype(x.dtype)


def split_in(proj):
    sizes = [RW_SHIFT, RW_WIDTH, LRU_WIDTH, LRU_WIDTH, XA_WIDTH, XA_WIDTH, N_BRANCH * D_MODEL]
    return jnp.split(proj, np.cumsum(sizes)[:-1].tolist(), axis=-1)


def rwkv7_branch(u, z, shift0, s0, mu, w0, w_decay, a0, w_aaa, k_k, k_a, r_k, gn_g, gn_b):
    f32 = jnp.float32
    bsz, t, _ = u.shape
    u = u.astype(f32)
    ext = jnp.concatenate([shift0.astype(f32)[:, None], u], axis=1)
    us = u + (ext[:, :-1] - u) * mu.astype(f32)
    new_shift = ext[:, -1]
    r, k, v, xw, xa = jnp.split(us, [RW_WIDTH, 2 * RW_WIDTH, 3 * RW_WIDTH, 3 * RW_WIDTH + DECAY_LORA], axis=-1)
    w = -jax.nn.softplus(-(w0.astype(f32) + jnp.tanh(xw) @ w_decay.astype(f32))) - 0.5
    decay = jnp.exp(-jnp.exp(w))
    a = jax.nn.sigmoid(a0.astype(f32) + xa @ w_aaa.astype(f32))
    heads = lambda y: y.reshape(bsz, t, RW_HEADS, RW_HEAD)
    kk = heads(k * k_k.astype(f32))
    kk = kk * lax.rsqrt(jnp.sum(kk * kk, axis=-1, keepdims=True) + L2_EPS)
    k = k * (1.0 + (a - 1.0) * k_a.astype(f32))
    r, k, v, decay, a = heads(r), heads(k), heads(v), heads(decay), heads(a)

    def step(s, inp):
        r_t, k_t, v_t, w_t, kk_t, a_t = inp
        sa = jnp.einsum("bhvk,bhk->bhv", s, -kk_t)
        s = (s * w_t[:, :, None, :] + sa[..., None] * (kk_t * a_t)[:, :, None, :]
             + v_t[..., None] * k_t[:, :, None, :])
        return s, jnp.einsum("bhvk,bhk->bhv", s, r_t)

    xs = tuple(jnp.moveaxis(y, 1, 0) for y in (r, k, v, decay, kk, a))
    s_last, o = lax.scan(step, s0.astype(f32), xs)
    o = jnp.moveaxis(o, 0, 1)
    mean = jnp.mean(o, axis=-1, keepdims=True)
    var = jnp.mean(jnp.square(o - mean), axis=-1, keepdims=True)
    o = ((o - mean) * lax.rsqrt(var + GN_EPS)).reshape(bsz, t, RW_WIDTH)
    o = o * gn_g.astype(f32) + gn_b.astype(f32)
    bonus = jnp.sum(r * k * r_k.astype(f32), axis=-1, keepdims=True) * v
    out = (o + bonus.reshape(bsz, t, RW_WIDTH)) * jax.nn.silu(z.astype(f32))
    return out, s_last, new_shift


def rglru_branch(xb, z, conv0, h0, conv_w, conv_b, w_rg_a, b_rg_a, w_rg_x, b_rg_x, lru_lambda):
    f32 = jnp.float32
    bsz, t, _ = xb.shape
    ext = jnp.concatenate([conv0.astype(f32), xb.astype(f32)], axis=1)
    cw = conv_w.astype(f32)
    xc = conv_b.astype(f32) + sum(ext[:, j:j + t] * cw[j] for j in range(CONV_W))
    new_conv = ext[:, t:]
    xblk = xc.reshape(bsz, t, LRU_BLOCKS, LRU_BLOCK)
    gate_r = jax.nn.sigmoid(jnp.einsum("btgi,gij->btgj", xblk, w_rg_a.astype(f32)).reshape(bsz, t, LRU_WIDTH)
                            + b_rg_a.astype(f32))
    gate_i = jax.nn.sigmoid(jnp.einsum("btgi,gij->btgj", xblk, w_rg_x.astype(f32)).reshape(bsz, t, LRU_WIDTH)
                            + b_rg_x.astype(f32))
    log_a = -LRU_C * gate_r * jax.nn.softplus(-lru_lambda.astype(f32))
    a = jnp.exp(log_a)
    b = jnp.sqrt(-jnp.expm1(2.0 * log_a)) * gate_i * xc

    def step(h, inp):
        a_t, b_t = inp
        h = a_t * h + b_t
        return h, h

    h_last, hs = lax.scan(step, h0.astype(f32), (jnp.moveaxis(a, 1, 0), jnp.moveaxis(b, 1, 0)))
    out = jnp.moveaxis(hs, 0, 1) * jax.nn.silu(z.astype(f32))
    return out, h_last, new_conv


def memory_kv(mem, g_mem, w_mem_kv):
    bsz = mem.shape[0]
    kv = rmsnorm(mem, g_mem) @ w_mem_kv
    mk, mv = jnp.split(kv, 2, axis=-1)
    return (mk.reshape(bsz, N_MEM, XA_HEADS, XA_HEAD), mv.reshape(bsz, N_MEM, XA_HEADS, XA_HEAD))


def cross_attn(q, z, mem_k, mem_v):
    f32 = jnp.float32
    bsz, t, _ = q.shape
    qh = q.astype(f32).reshape(bsz, t, XA_HEADS, XA_HEAD)
    s = jnp.einsum("bthd,bmhd->bhtm", qh, mem_k.astype(f32)) * (XA_HEAD ** -0.5)
    p = jax.nn.softmax(s, axis=-1)
    o = jnp.einsum("bhtm,bmhd->bthd", p, mem_v.astype(f32)).reshape(bsz, t, XA_WIDTH)
    return o * jax.nn.silu(z.astype(f32))


def mixer_layer(x, mem_k, mem_v, s0, shift0, h0, conv0, g_norm, w_in, mu_shift, w0, w_decay, a0, w_aaa,
                k_k, k_a, r_k, gn_g, gn_b, conv_w, conv_b, w_rg_a, b_rg_a, w_rg_x, b_rg_x, lru_lambda,
                w_br_a, w_br_b, w_br_c, w_out):
    f32 = jnp.float32
    xn = rmsnorm(x, g_norm)
    u, za, xb, zb, q, zc, gates = split_in(xn @ w_in)
    oa, s_last, shift_last = rwkv7_branch(u, za, shift0, s0, mu_shift, w0, w_decay, a0, w_aaa,
                                          k_k, k_a, r_k, gn_g, gn_b)
    ob, h_last, conv_last = rglru_branch(xb, zb, conv0, h0, conv_w, conv_b, w_rg_a, b_rg_a,
                                         w_rg_x, b_rg_x, lru_lambda)
    oc = cross_attn(q, zc, mem_k, mem_v)
    g_a, g_b, g_c = jnp.split(jax.nn.sigmoid(gates.astype(f32)), N_BRANCH, axis=-1)
    merged = (g_a * (oa @ w_br_a.astype(f32)) + g_b * (ob @ w_br_b.astype(f32))
              + g_c * (oc @ w_br_c.astype(f32)))
    y = x + (merged @ w_out.astype(f32)).astype(x.dtype)
    return y, s_last, shift_last, h_last, conv_last


def setup_inputs(seed: int = 0) -> dict:
    key = jax.random.key(seed)
    ks = iter(jax.random.split(key, 40))
    nrm = lambda shape, scale: scale * jax.random.normal(next(ks), shape, jnp.float32)
    uni = lambda shape, lo, hi: jax.random.uniform(next(ks), shape, jnp.float32, lo, hi)
    L = DEPTH
    a_target = uni((L, LRU_WIDTH), 0.9, 0.999)
    sig = a_target ** (1.0 / LRU_C)
    lru_lambda = jnp.log(sig) - jnp.log1p(-sig)
    return {
        "x_prompt": nrm((BATCH, SEQ, D_MODEL), 1.0),
        "x_sample": nrm((DEC_BATCH, DEC_SEQ, D_MODEL), 1.0),
        "mem_prompt": nrm((BATCH, N_MEM, D_MODEL), 1.0),
        "state_rwkv": nrm((L, DEC_BATCH, RW_HEADS, RW_HEAD, RW_HEAD), 0.3),
        "state_shift": nrm((L, DEC_BATCH, RW_SHIFT), 1.0),
        "state_lru": nrm((L, DEC_BATCH, LRU_WIDTH), 0.5),
        "state_conv": nrm((L, DEC_BATCH, CONV_W - 1, LRU_WIDTH), 1.0),
        "cache_mem_k": nrm((L, DEC_BATCH, N_MEM, XA_HEADS, XA_HEAD), 1.0),
        "cache_mem_v": nrm((L, DEC_BATCH, N_MEM, XA_HEADS, XA_HEAD), 1.0),
        "g_norm": 1.0 + nrm((L, D_MODEL), 0.02),
        "w_in": nrm((L, D_MODEL, IN_COLS), D_MODEL ** -0.5),
        "mu_shift": uni((L, RW_SHIFT), 0.0, 1.0),
        "w0": uni((L, RW_WIDTH), -5.0, 1.0),
        "w_decay": nrm((L, DECAY_LORA, RW_WIDTH), 0.1),
        "a0": nrm((L, RW_WIDTH), 0.1),
        "w_aaa": nrm((L, AAA_LORA, RW_WIDTH), 0.5 * AAA_LORA ** -0.5),
        "k_k": 0.85 + nrm((L, RW_WIDTH), 0.05),
        "k_a": 1.0 + nrm((L, RW_WIDTH), 0.05),
        "r_k": nrm((L, RW_HEADS, RW_HEAD), 0.1),
        "gn_g": 1.0 + nrm((L, RW_WIDTH), 0.02),
        "gn_b": nrm((L, RW_WIDTH), 0.02),
        "conv_w": nrm((L, CONV_W, LRU_WIDTH), 0.5),
        "conv_b": nrm((L, LRU_WIDTH), 0.02),
        "w_rg_a": nrm((L, LRU_BLOCKS, LRU_BLOCK, LRU_BLOCK), LRU_BLOCK ** -0.5),
        "b_rg_a": nrm((L, LRU_WIDTH), 0.02),
        "w_rg_x": nrm((L, LRU_BLOCKS, LRU_BLOCK, LRU_BLOCK), LRU_BLOCK ** -0.5),
        "b_rg_x": nrm((L, LRU_WIDTH), 0.02),
        "lru_lambda": lru_lambda,
        "g_mem": 1.0 + nrm((L, D_MODEL), 0.02),
        "w_mem_kv": nrm((L, D_MODEL, 2 * XA_WIDTH), D_MODEL ** -0.5),
        "w_br_a": nrm((L, RW_WIDTH, D_MODEL), RW_WIDTH ** -0.5),
        "w_br_b": nrm((L, LRU_WIDTH, D_MODEL), LRU_WIDTH ** -0.5),
        "w_br_c": nrm((L, XA_WIDTH, D_MODEL), XA_WIDTH ** -0.5),
        "w_out": nrm((L, D_MODEL, D_MODEL), D_MODEL ** -0.5),
        "g_final": 1.0 + nrm((D_MODEL,), 0.02),
    }


def reference(x_prompt, x_sample, mem_prompt, state_rwkv, state_shift, state_lru, state_conv,
              cache_mem_k, cache_mem_v, g_norm, w_in, mu_shift, w0, w_decay, a0, w_aaa, k_k, k_a, r_k,
              gn_g, gn_b, conv_w, conv_b, w_rg_a, b_rg_a, w_rg_x, b_rg_x, lru_lambda, g_mem, w_mem_kv,
              w_br_a, w_br_b, w_br_c, w_out, g_final):
    f32 = jnp.float32
    bp = x_prompt.shape[0]
    hp, hs = x_prompt, x_sample
    rw_p, sh_p, lru_p, cv_p, mk_p, mv_p, rw_s, sh_s, lru_s, cv_s = ([] for _ in range(10))
    for l in range(DEPTH):
        lw = (g_norm[l], w_in[l], mu_shift[l], w0[l], w_decay[l], a0[l], w_aaa[l], k_k[l], k_a[l], r_k[l],
              gn_g[l], gn_b[l], conv_w[l], conv_b[l], w_rg_a[l], b_rg_a[l], w_rg_x[l], b_rg_x[l],
              lru_lambda[l], w_br_a[l], w_br_b[l], w_br_c[l], w_out[l])
        mk, mv = memory_kv(mem_prompt, g_mem[l], w_mem_kv[l])
        hp, s1, s2, s3, s4 = mixer_layer(
            hp, mk, mv,
            jnp.zeros((bp, RW_HEADS, RW_HEAD, RW_HEAD), f32), jnp.zeros((bp, RW_SHIFT), f32),
            jnp.zeros((bp, LRU_WIDTH), f32), jnp.zeros((bp, CONV_W - 1, LRU_WIDTH), f32), *lw)
        rw_p.append(s1); sh_p.append(s2); lru_p.append(s3); cv_p.append(s4); mk_p.append(mk); mv_p.append(mv)
        hs, t1, t2, t3, t4 = mixer_layer(
            hs, cache_mem_k[l], cache_mem_v[l], state_rwkv[l], state_shift[l], state_lru[l], state_conv[l], *lw)
        rw_s.append(t1); sh_s.append(t2); lru_s.append(t3); cv_s.append(t4)
    y_prompt = rmsnorm(hp, g_final)
    y_sample = rmsnorm(hs, g_final)
    stk = lambda xs: jnp.stack(xs, axis=0)
    return (y_prompt, y_sample, stk(rw_p), stk(sh_p), stk(lru_p), stk(cv_p), stk(mk_p), stk(mv_p),
            stk(rw_s), stk(sh_s), stk(lru_s), stk(cv_s))
```

```python
import numpy as np
from contextlib import ExitStack
import concourse.bass as bass
import concourse.mybir as mybir
from concourse.bass_utils import run_bass_kernel_spmd

F32 = mybir.dt.float32
BF16 = mybir.dt.bfloat16
AF = mybir.ActivationFunctionType
ALU = mybir.AluOpType
AX = mybir.AxisListType

NCORE = 8
TP = 2048
NSEQ = 16
SLEN = 8
TS = NSEQ * SLEN
NTOK = TP + TS
NCOL = 11392
C0 = float(np.exp(-0.5))
ENABLE_RWKV = True
DEBUG = False
DBG_OUT = {}


class Op:
    __slots__ = ("eng", "fn", "deps", "dma", "sem", "semval", "sig", "sigidx", "seq", "gidx", "t_end")


class Prog:
    COMPUTE = ("pe", "act", "dve", "pool")

    def __init__(self, nc, n_dma_sems=14):
        self.nc = nc
        self.ops = []
        self.lastw = {}
        self.rd_eng = {}
        self.rd_dma = {}
        self.seq = {e: 0 for e in ("pe", "act", "dve", "pool", "sp")}
        self.n_dma_sems = n_dma_sems
        self.dma_count = {}
        self.eng_free = {e: 0.0 for e in ("pe", "act", "dve", "pool", "sp")}
        self.pe_last = {}
        self.mark_end = 0.0

    def add(self, eng, fn, reads=(), writes=(), dma=False, cost=0.3, rowtile=None):
        op = Op()
        op.eng = eng; op.fn = fn; op.dma = dma; op.sig = False; op.sigidx = 0
        op.sem = None; op.semval = 0
        op.seq = self.seq[eng]; self.seq[eng] += 1
        op.gidx = len(self.ops)
        writes = list(writes) + [k for k in reads if isinstance(k, str) and k.startswith("ps") and k not in writes]
        deps = {}

        def dep(p, raw):
            if p is None or p is op:
                return
            if (not dma) and (not p.dma) and p.eng == eng:
                if eng == "pe":
                    return
            deps[p.gidx] = p
        for k in reads:
            dep(self.lastw.get(k), True)
        for k in writes:
            dep(self.lastw.get(k), False)
            for r in self.rd_eng.get(k, {}).values():
                dep(r, False)
            for r in self.rd_dma.get(k, ()):
                dep(r, False)
        if eng == "pe" and rowtile is not None:
            for k in writes:
                if isinstance(k, str) and k.startswith("ps"):
                    last = self.pe_last.get(k)
                    if last is not None and (last[1][0] + last[1][1] <= rowtile[0] or rowtile[0] + rowtile[1] <= last[1][0]) and op.seq - last[0].seq <= 12:
                        deps[last[0].gidx] = last[0]
                    self.pe_last[k] = (op, rowtile)
        op.deps = list(deps.values())
        for p in op.deps:
            if not p.dma:
                p.sig = True
        t0_ = self.eng_free[eng]
        for p in op.deps:
            t0_ = max(t0_, p.t_end + 0.3)
        if dma:
            self.eng_free[eng] = t0_ + 0.1
            op.t_end = t0_ + cost
        else:
            op.t_end = t0_ + cost
            self.eng_free[eng] = op.t_end
        self.mark_end = max(self.mark_end, op.t_end)
        wset = set(writes)
        for k in writes:
            self.lastw[k] = op
            self.rd_eng[k] = {}
            self.rd_dma[k] = []
        for k in reads:
            if k in wset:
                continue
            if dma:
                self.rd_dma.setdefault(k, []).append(op)
            else:
                self.rd_eng.setdefault(k, {})[eng] = op
        if dma:
            i = self.dma_count.get(eng, 0)
            self.dma_count[eng] = i + 1
            op.sem = (eng, i % self.n_dma_sems)
            op.semval = 16 * (i // self.n_dma_sems + 1)
        self.ops.append(op)
        return op

    def emit(self, es):
        nc = self.nc
        esem = {e: es.enter_context(nc.semaphore("s_" + e)) for e in self.COMPUTE}
        dsem = {}
        for q in self.dma_count:
            for j in range(self.n_dma_sems):
                dsem[(q, j)] = es.enter_context(nc.semaphore("d_%s_%d" % (q, j)))
        cnt = {e: 0 for e in self.COMPUTE}
        for op in self.ops:
            if (not op.dma) and op.sig:
                cnt[op.eng] += 1
                op.sigidx = cnt[op.eng]
        per = {e: [] for e in ("pe", "act", "dve", "pool", "sp")}
        for op in self.ops:
            per[op.eng].append(op)
        final = {}
        for op in self.ops:
            if op.dma:
                final[op.sem] = max(final.get(op.sem, 0), op.semval)
        block = es.enter_context(nc.Block())

        def run(ops, is_last):
            def body(h):
                waited = {}

                def w(sem, key, val):
                    if waited.get(key, 0) >= val:
                        return
                    h.wait_ge(sem, val)
                    waited[key] = val
                for op in ops:
                    for p in op.deps:
                        if p.dma:
                            w(dsem[p.sem], p.sem, p.semval)
                        else:
                            w(esem[p.eng], p.eng, p.sigidx)
                    if op.dma:
                        if op.semval > 16:
                            w(dsem[op.sem], op.sem, op.semval - 16)
                        ins = op.fn(h)
                        ins.then_inc(dsem[op.sem], 16)
                    else:
                        ins = op.fn(h)
                        if op.sig:
                            ins.then_inc(esem[op.eng], 1)
                if is_last:
                    for k, v in final.items():
                        w(dsem[k], k, v)
            return body

        block.tensor(run(per["pe"], False))
        block.scalar(run(per["act"], False))
        block.vector(run(per["dve"], False))
        block.gpsimd(run(per["pool"], False))
        block.sync(run(per["sp"], True))


def _colperm():
    perm = list(range(3072, 3200))
    for j in range(8):
        for base in (0, 1024, 2048, 3200):
            perm += list(range(base + j * 128, base + (j + 1) * 128))
    for j in range(8):
        for base in (4224, 5248):
            perm += list(range(base + j * 128, base + (j + 1) * 128))
    for j in range(8):
        for base in (6272, 7296):
            perm += list(range(base + j * 128, base + (j + 1) * 128))
    for m in range(8):
        for base in (8320, 9344, 10368):
            perm += list(range(base + m * 128, base + (m + 1) * 128))
    assert len(perm) == NCOL
    return np.array(perm)


OFF_LORA = 0
OFF_RW = 128
OFF_LRU = 128 + 4096
OFF_XA = OFF_LRU + 2048
OFF_GATE = OFF_XA + 2048

PV_NAMES = ["g_norm", "mu_r", "mu_k", "mu_v", "w0", "a0", "k_k", "k_a", "r_k", "gn_g", "gn_b",
            "cw0", "cw1", "cw2", "cw3", "conv_b", "b_rg_a", "b_rg_x", "lam", "g_mem", "g_final"]
PVI = {n: 8 * i for i, n in enumerate(PV_NAMES)}
PV_MULORA = 8 * len(PV_NAMES)
NPV = PV_MULORA + 1
DV_NAMES = ["omm_r", "omm_k", "omm_v", "omka", "gn32", "gm32", "gf32", "clam", "clam2"]
DVI = {n: 8 * i for i, n in enumerate(DV_NAMES)}
DV_OMMLORA = 8 * len(DV_NAMES)
NDV = DV_OMMLORA + 1


def _fm(v):
    return np.ascontiguousarray(np.asarray(v, np.float32).reshape(8, 128).T)


def build_nc():
    nc = bass.Bass("TRN2", target_bir_lowering=False)
    din = lambda name, shape: nc.dram_tensor(name, shape, F32, kind="ExternalInput").ap()
    dout = lambda name, shape: nc.dram_tensor(name, shape, F32, kind="ExternalOutput").ap()
    d_xT = din("xT", [128, 8, NTOK])
    d_memT = din("memT", [128, 8, 256])
    d_win = din("w_in_l", [128, 8, NCOL])
    d_wkv = din("w_kv_l", [128, 8, 2048])
    d_wbr = din("w_br_l", [128, 8, 3, 8, 128])
    d_wout = din("w_out_l", [128, 8, 8, 128])
    d_wl = din("wl", [128, 1024])
    d_wrga = din("w_rg_a", [16, 64, 64])
    d_wrgx = din("w_rg_x", [16, 64, 64])
    d_pvec = din("pvec", [128, NPV])
    d_kTs = din("kT_s", [NSEQ, 128, 8, 256])
    d_vs = din("v_s", [NSEQ, 128, 2, 1024])
    d_strw = din("st_rwkv", [128, 8, NSEQ, 64])
    d_stsh = din("st_shift", [128, 25, NSEQ])
    d_stlru = din("st_lru", [128, 8, NSEQ])
    d_stcv = din("st_conv", [128, 8, NSEQ, 3])
    o_yT = dout("yT", [128, 8, NTOK])
    o_rwp = dout("rwkv_p", [128, 8, 1, 64])
    o_shp = dout("shift_p", [128, 25, 1])
    o_lrup = dout("lru_p", [128, 8, 1])
    o_cvp = dout("conv_p", [128, 8, 1, 3])
    o_mk = dout("memk_p", [128, 8, 256])
    o_mv = dout("memv_p", [128, 2, 1024])
    o_rws = dout("rwkv_s", [128, 8, NSEQ, 64])
    o_shs = dout("shift_s", [128, 25, NSEQ])
    o_lrus = dout("lru_s", [128, 8, NSEQ])
    o_cvs = dout("conv_s", [128, 8, NSEQ, 3])

    with ExitStack() as es:
        P = Prog(nc)
        sb = lambda name, shape, dt=F32: es.enter_context(nc.sbuf_tensor(name, shape, dt))
        pst = lambda name, shape, dt=F32: es.enter_context(nc.psum_tensor(name, shape, dt))

        def fsz(ap):
            n_ = 1
            for d_ in ap.shape[1:]:
                n_ *= int(d_)
            return n_

        def dma(q, out, in_, R=(), W=()):
            return P.add(q, lambda h: h.dma_start(out=out, in_=in_), reads=R, writes=W, dma=True, cost=2.0 + fsz(out) * 128 * 4 / 150e3)

        def act(out, in_, func, R, W, bias=None, scale=None, accum=None):
            kw = {}
            if bias is not None:
                kw["bias"] = bias
            if scale is not None:
                kw["scale"] = scale
            if accum is not None:
                kw["accum_out"] = accum
            return P.add("act", lambda h: h.activation(out=out, in_=in_, func=func, **kw), reads=R, writes=W, cost=0.2 + fsz(out) / 1200.0)

        def tt(eng, out, in0, in1, op, R, W):
            return P.add(eng, lambda h: h.tensor_tensor(out=out, in0=in0, in1=in1, op=op), reads=R, writes=W,
                         cost=(0.12 + fsz(out) / 960.0) if eng == "dve" else (0.25 + fsz(out) / 450.0))

        def ts(eng, out, in0, s1, op0, R, W, s2=None, op1=None):
            c_ = (0.12 + fsz(out) / 960.0) if eng == "dve" else (0.3 + fsz(out) / 80.0)
            if op1 is None:
                return P.add(eng, lambda h: h.tensor_scalar(out=out, in0=in0, scalar1=s1, scalar2=None, op0=op0), reads=R, writes=W, cost=c_)
            return P.add(eng, lambda h: h.tensor_scalar(out=out, in0=in0, scalar1=s1, scalar2=s2, op0=op0, op1=op1), reads=R, writes=W, cost=c_)

        def stt(out, in0, scalar, in1, op0, op1, R, W):
            return P.add("dve", lambda h: h.scalar_tensor_tensor(out=out, in0=in0, scalar=scalar, in1=in1, op0=op0, op1=op1), reads=R, writes=W,
                         cost=0.12 + fsz(out) / 960.0)

        def cp(eng, out, in_, R, W):
            if eng == "act":
                return act(out, in_, AF.Copy, R, W)
            return P.add(eng, lambda h: h.tensor_copy(out=out, in_=in_), reads=R, writes=W,
                         cost=(0.12 + fsz(out) / 960.0) if eng == "dve" else (0.25 + fsz(out) / 450.0))

        def mm(out, lhsT, rhs, start, stop, R, W):
            return P.add("pe", lambda h: h.matmul(out, lhsT=lhsT, rhs=rhs, start=start, stop=stop), reads=R, writes=W,
                         cost=(0.035 + fsz(rhs) / 2400.0) * (4.0 if rhs.dtype == F32 else 1.0),
                         rowtile=(lhsT.base_partition(), 128 if lhsT.shape[0] > 64 else (64 if lhsT.shape[0] > 32 else 32)))

        def tr(out, in_, ident, R, W):
            return P.add("pe", lambda h: h.transpose(out=out, in_=in_, identity=ident), reads=R, writes=W, cost=0.05,
                         rowtile=(in_.base_partition(), 128 if in_.shape[0] > 64 else (64 if in_.shape[0] > 32 else 32)))

        def scan(out, d0, d1, R, W):
            return P.add("dve", lambda h: h.tensor_tensor_scan(out=out, data0=d0, data1=d1, initial=0.0, op0=ALU.mult, op1=ALU.add), reads=R, writes=W,
                         cost=0.1 + 2.0 * fsz(out) / 960.0)

        def memset(eng, ap, val, W):
            return P.add(eng, lambda h: h.memset(ap, val), writes=W)

        psM = [pst("psM%d" % i, [128, 512]) for i in range(2)]
        psA = [pst("psA%d" % i, [128, 512]) for i in range(4)]
        psT = [pst("psT%d" % i, [128, 1024], BF16) for i in range(2)]
        rot = {"M": 0, "A": 0, "T": 0, "AP": 0, "A3": 0}

        psTf = [psT[i][:].bitcast(F32) for i in range(2)]
        MNARROW = [(psM[0], "psM0"), (psTf[0], "psT0"), (psM[1], "psM1"), (psTf[1], "psT1")]
        MWIDE = MNARROW + [(psA[i], "psA%d" % i) for i in range(4)]
        mlist = [MNARROW]

        def bank(kind):
            if kind == "M":
                lst_ = mlist[0]
                i = rot[kind] % len(lst_)
                rot[kind] += 1
                return lst_[i]
            lst = {"A": psA, "T": psT}[kind]
            i = rot[kind] % len(lst)
            rot[kind] += 1
            return lst[i], "ps%s%d" % (kind, i)

        def bankpair():
            i = (rot["AP"] % 2) * 2
            rot["AP"] += 1
            return [(psA[i], "psA%d" % i), (psA[i + 1], "psA%d" % (i + 1))]

        pv = sb("pv", [128, NPV])
        dv = sb("dv", [128, NDV])
        ident = sb("ident", [128, 128], BF16)
        onesb = sb("onesb", [128, 128], BF16)
        bd1 = sb("bd1", [128, 128])
        bd64 = sb("bd64", [128, 128])
        bd1b = sb("bd1b", [128, 128], BF16)
        bd64b = sb("bd64b", [128, 128], BF16)
        onesf = sb("onesf", [128, 1, 64])
        masks = {}
        for Cc in (64, 8):
            for nm in ("sl", "su", "ue", "eq"):
                masks[(nm, Cc)] = sb("mk_%s_%d" % (nm, Cc), [128, 1, Cc])
        segmask = {64: sb("segm64", [128, 512]), 8: sb("segm8", [128, 128])}
        wlb = sb("wlb", [128, 1024], BF16)
        wrg = sb("wrg", [128, 8, 2, 128], BF16)
        epsn = sb("epsn", [128, 4])

        dma("sp", pv[:], d_pvec, W=["pv"])
        memset("pool", onesb[:], 1.0, ["onesb"])
        memset("pool", onesf[:], 1.0, ["onesf"])
        P.add("pool", lambda h: h.affine_select(out=ident[:], in_=onesb[:], pattern=[[-1, 128]], compare_op=ALU.is_equal,
                                                fill=0.0, base=0, channel_multiplier=1), reads=["onesb"], writes=["ident"])
        memset("dve", bd1[:], 0.0, ["bd1"])
        memset("dve", bd64[:], 0.0, ["bd64"])
        for hh in range(2):
            memset("dve", bd1[64 * hh:64 * hh + 64, 64 * hh:64 * hh + 64], 1.0, ["bd1"])
            memset("dve", bd64[64 * hh:64 * hh + 64, 64 * hh:64 * hh + 64], 1.0 / 64.0, ["bd64"])
        cp("dve", bd1b[:], bd1[:], ["bd1"], ["bd1b"])
        cp("dve", bd64b[:], bd64[:], ["bd64"], ["bd64b"])
        memset("dve", epsn[:, 0:1], 1e-12, ["epsn"])
        memset("dve", epsn[:, 1:2], 64e-5, ["epsn"])
        memset("dve", epsn[:, 2:3], 1024.0 * 1e-6, ["epsn"])
        memset("dve", epsn[:, 3:4], 1.0, ["epsn"])
        for Cc in (64, 8):
            for hh in range(2):
                sl_ = slice(64 * hh, 64 * hh + 64)
                P.add("pool", lambda h, sl_=sl_, Cc=Cc: h.affine_select(out=masks[("sl", Cc)][sl_], in_=onesf[sl_, :, 0:Cc], pattern=[[0, 1], [-1, Cc]],
                      compare_op=ALU.is_gt, fill=0.0, base=0, channel_multiplier=1), reads=["onesf"], writes=["mk_sl_%d" % Cc])
                P.add("pool", lambda h, sl_=sl_, Cc=Cc: h.affine_select(out=masks[("su", Cc)][sl_], in_=onesf[sl_, :, 0:Cc], pattern=[[0, 1], [1, Cc]],
                      compare_op=ALU.is_gt, fill=0.0, base=0, channel_multiplier=-1), reads=["onesf"], writes=["mk_su_%d" % Cc])
                P.add("pool", lambda h, sl_=sl_, Cc=Cc: h.affine_select(out=masks[("ue", Cc)][sl_], in_=onesf[sl_, :, 0:Cc], pattern=[[0, 1], [1, Cc]],
                      compare_op=ALU.is_ge, fill=0.0, base=0, channel_multiplier=-1), reads=["onesf"], writes=["mk_ue_%d" % Cc])
                P.add("pool", lambda h, sl_=sl_, Cc=Cc: h.affine_select(out=masks[("eq", Cc)][sl_], in_=onesf[sl_, :, 0:Cc], pattern=[[0, 1], [1, Cc]],
                      compare_op=ALU.is_equal, fill=0.0, base=0, channel_multiplier=-1), reads=["onesf"], writes=["mk_eq_%d" % Cc])
        for Cc, n_ in ((64, 512), (8, 128)):
            memset("pool", segmask[Cc][:], 1.0, ["segm%d" % Cc])
            memset("pool", segmask[Cc][:].rearrange("p (a b) -> p a b", b=Cc)[:, :, 0:1], 0.0, ["segm%d" % Cc])
        for nm, src in (("omm_r", "mu_r"), ("omm_k", "mu_k"), ("omm_v", "mu_v"), ("omka", "k_a")):
            ts("dve", dv[:, DVI[nm]:DVI[nm] + 8], pv[:, PVI[src]:PVI[src] + 8], -1.0, ALU.mult, ["pv"], ["dv"], s2=1.0, op1=ALU.add)
        ts("dve", dv[:, DV_OMMLORA:DV_OMMLORA + 1], pv[:, PV_MULORA:PV_MULORA + 1], -1.0, ALU.mult, ["pv"], ["dv"], s2=1.0, op1=ALU.add)
        for nm, src in (("gn32", "g_norm"), ("gm32", "g_mem"), ("gf32", "g_final")):
            ts("dve", dv[:, DVI[nm]:DVI[nm] + 8], pv[:, PVI[src]:PVI[src] + 8], 32.0, ALU.mult, ["pv"], ["dv"])
        act(dv[:, DVI["clam"]:DVI["clam"] + 8], pv[:, PVI["lam"]:PVI["lam"] + 8], AF.Exp, ["pv"], ["dv"], scale=-1.0)
        act(dv[:, DVI["clam"]:DVI["clam"] + 8], dv[:, DVI["clam"]:DVI["clam"] + 8], AF.Ln, ["dv"], ["dv"], bias=epsn[:, 3:4], scale=1.0)
        ts("dve", dv[:, DVI["clam"]:DVI["clam"] + 8], dv[:, DVI["clam"]:DVI["clam"] + 8], -8.0, ALU.mult, ["dv"], ["dv"])
        ts("dve", dv[:, DVI["clam2"]:DVI["clam2"] + 8], dv[:, DVI["clam"]:DVI["clam"] + 8], 2.0, ALU.mult, ["dv"], ["dv"])
        dma("pool", wlb[:], d_wl, W=["wlb"])
        memset("dve", wrg[:], 0.0, ["wrg"])
        for g, src in ((0, d_wrga), (1, d_wrgx)):
            for hh in range(2):
                s_ = src.rearrange("(j h) i o -> h i j o", h=2)[hh]
                dma("pool", wrg[64 * hh:64 * hh + 64, :, g, 64 * hh:64 * hh + 64], s_, W=["wrg"])

        def pcol(name, c):
            i = PVI[name] + c
            return pv[:, i:i + 1]

        def dcol(name, c):
            i = DVI[name] + c
            return dv[:, i:i + 1]

        NMAX = 512
        xT = sb("xTs", [128, 8, NMAX])
        xnT = sb("xnTs", [128, 8, NMAX], BF16)
        oT = [sb("oT%d" % b, [128, 8, NMAX], BF16) for b in range(3)]
        mgT = sb("mgT", [128, 8, NMAX], BF16)
        NF, NB = 18, 41
        Ft = [sb("F%d" % i, [128, 520]) for i in range(NF)]
        Bt = [sb("B%d" % i, [128, 520], BF16) for i in range(NB)]
        NFA = 8
        XK = ["xTf%d" % i for i in range(NFA)]
        freeF = list(range(NF))
        freeFA = list(range(NFA))
        freeB = list(range(NB))

        class T:
            def __init__(self, t, key, idx, pool):
                self.t, self.key, self.idx, self.pool = t, key, idx, pool

            def free(self):
                {"F": freeF, "B": freeB, "FA": freeFA}[self.pool].append(self.idx)

        def fal(wide=False):
            if (not wide) and ALIAS_OK[0] and freeFA:
                i = freeFA.pop(0)
                return T(xT[:, i, :], "xTf%d" % i, i, "FA")
            i = freeF.pop(0)
            return T(Ft[i], "F%d" % i, i, "F")
        ALIAS_OK = [False]

        def bal():
            i = freeB.pop(0)
            return T(Bt[i], "B%d" % i, i, "B")

        class WS:
            def __init__(self, names, width, lookahead):
                self.bufs = [sb(nm, [128, width], BF16) for nm in names]
                self.keys = list(names)
                self.list = []
                self.ptr = 0
                self.loaded = 0
                self.look = lookahead

            def add(self, src, nfree, seg=0):
                self.list.append((src, nfree, seg))

            def load(self, i):
                src, nfree, _ = self.list[i]
                b_ = i % len(self.bufs)
                dst = self.bufs[b_][:, 0:nfree]
                shp = src.shape
                if len(shp) == 3:
                    dst = dst.rearrange("p (a b) -> p a b", a=shp[1])
                elif len(shp) == 4:
                    dst = dst.rearrange("p (a b c) -> p a b c", a=shp[1], b=shp[2])
                dma("pool", dst, src, W=[self.keys[b_]])

            def next(self):
                i = self.ptr
                seg = self.list[i][2]
                while self.loaded < len(self.list) and self.loaded <= i + self.look and (self.loaded <= i or self.list[self.loaded][2] == seg):
                    self.load(self.loaded)
                    self.loaded += 1
                self.ptr += 1
                b_ = i % len(self.bufs)
                return self.bufs[b_], self.keys[b_]

        wR = WS(["wR0", "wR1"], 4096, 1)
        lm_names = ["wLM0", "wLM1", "wLM2"]
        wL = WS(lm_names, 3072, 2)
        wM = WS([], 3072, 1)
        wM.bufs = wL.bufs
        wM.keys = wL.keys

        st = {}
        for kind, ns in (("p", 1), ("s", NSEQ)):
            st[kind] = dict(
                U=sb("cU_" + kind, [128, 25, ns]),
                H=sb("cH_" + kind, [128, 8, ns]),
                CV=sb("cCV_" + kind, [128, 8, ns, 3]),
            )
        st["p"]["RW"] = sb("cRW_p", [128, 8, 1, 64])
        rwj = [sb("rwj%d" % i, [128, NSEQ, 64]) for i in range(2)]
        for nm in ("U", "H", "CV", "RW"):
            memset("pool", st["p"][nm][:], 0.0, ["c%s_p" % nm])
        dma("sp", st["s"]["U"][:], d_stsh, W=["cU_s"])
        dma("sp", st["s"]["H"][:], d_stlru, W=["cH_s"])
        dma("sp", st["s"]["CV"][:], d_stcv, W=["cCV_s"])

        KT = [sb("KT%d" % i, [128, 8, 256], BF16) for i in range(1)]
        VT = [sb("VT%d" % i, [128, 2, 1024], BF16) for i in range(1)]
        NKS = 2
        KTs = [sb("KTs%d" % i, [128, 2, 256], BF16) for i in range(NKS)]
        VTs = [sb("VTs%d" % i, [128, 2, 256], BF16) for i in range(NKS)]
        ksrot = [0]

        sts = [("p", i * 512, 512) for i in range(4)] + [("s", TP, TS)]

        def wsrc(off, n):
            return d_win[:, :, off:off + n], 8 * n
        for i in range(4):
            wR.add(d_wkv[:, :, i * 512:(i + 1) * 512], 8 * 512)
        for si_, _ in enumerate(sts):
            wR.add(*wsrc(OFF_LORA, 128))
            for j in range(8):
                wR.add(*wsrc(OFF_RW + j * 512, 512))
            for j in range(8):
                wL.add(*wsrc(OFF_LRU + j * 256, 256), seg=si_)
            for j in range(8):
                wL.add(*wsrc(OFF_XA + j * 256, 256), seg=si_)
            for m in range(8):
                wM.add(*wsrc(OFF_GATE + m * 384, 384), seg=si_)
                wM.add(d_wbr[:, m], 3 * 8 * 128, seg=si_)
            for oc in range(8):
                wM.add(d_wout[:, oc], 8 * 128, seg=si_)

        def wview(wt, n):
            return wt[:, 0:8 * n].rearrange("p (k c) -> p k c", c=n)

        def proj(wv, col0, n, wkey, ncols=128):
            pb, pk = bank("M")
            for kc in range(8):
                mm(pb[0:ncols, 0:n], wv[:, kc, col0:col0 + ncols], xnT[:, kc, 0:n], kc == 0, kc == 7, [wkey, "xnT"], [pk])
            return pb, pk

        def rsqrt_from(out, in_, eps_col, R, W):
            act(out, in_, AF.Ln, R, W, bias=epsn[:, eps_col:eps_col + 1], scale=1.0)
            act(out, out, AF.Exp, W, W, scale=-0.5)

        def norm_rinv(src_t, src_key, n, nchunk=8):
            sqs = [bal(), bal()]
            pa, pak = bank("A")
            for kc in range(nchunk):
                sq = sqs[kc % 2]
                sqv = sq.t[:, 0:n]
                act(sqv, src_t[:, kc, 0:n], AF.Square, [src_key[kc] if isinstance(src_key, list) else src_key], [sq.key])
                mm(pa[:, 0:n], onesb[:], sqv, kc == 0, kc == nchunk - 1, [sq.key, "onesb"], [pak])
            r = fal()
            rsqrt_from(r.t[:, 0:n], pa[:, 0:n], 2, [pak], [r.key])
            sqs[0].free(); sqs[1].free()
            return r

        for kc in range(8):
            dma("sp", xT[:, kc, 0:256], d_memT[:, kc, :], W=[XK[kc]])
        rm = norm_rinv(xT, XK, 256)
        for kc in range(8):
            stt(xnT[:, kc, 0:256], xT[:, kc, 0:256], dcol("gm32", kc), rm.t[:, 0:256], ALU.mult, ALU.mult, [XK[kc], "dv", rm.key], ["xnT"])
        rm.free()
        for i in range(2):
            wt, wk = wR.next()
            wv = wview(wt, 512)
            for q in range(4):
                oc = i * 4 + q
                pb, pk = bank("M")
                for kc in range(8):
                    mm(pb[:, 0:256], wv[:, kc, q * 128:(q + 1) * 128], xnT[:, kc, 0:256], kc == 0, kc == 7, [wk, "xnT"], [pk])
                kf = fal()
                cp("act", kf.t[:, 0:256], pb[:, 0:256], [pk], [kf.key])
                cp("dve", KT[0][:, oc, :], pb[:, 0:256], [pk], ["KT0"])
                dma("sp", o_mk[:, oc, :], kf.t[:, 0:256], R=[kf.key])
                kf.free()
        for i in range(2):
            wt, wk = wR.next()
            wv = wview(wt, 512)
            for mh in range(2):
                pb, pk = bank("M")
                for kc in range(8):
                    mm(pb[:, 0:512], xnT[:, kc, mh * 128:(mh + 1) * 128], wv[:, kc, :], kc == 0, kc == 7, [wk, "xnT"], [pk])
                vf = fal()
                cp("act", vf.t[:, 0:512], pb[:, 0:512], [pk], [vf.key])
                cp("dve", VT[0][:, mh, i * 512:(i + 1) * 512], pb[:, 0:512], [pk], ["VT0"])
                dma("sp", o_mv[:, mh, i * 512:(i + 1) * 512], vf.t[:, 0:512], R=[vf.key])
                vf.free()

        for (kind, tok0, n) in sts:
            sample = kind == "s"
            NSEG = NSEQ if sample else 1
            SL = SLEN if sample else 512
            C = 8 if sample else 64
            NCH = n // C
            S = st[kind]
            kU, kH, kCV = "cU_" + kind, "cH_" + kind, "cCV_" + kind

            def v3(ap, a=NSEG):
                return ap.rearrange("p (a b) -> p a b", a=a)

            for kc in range(8):
                dma("sp", xT[:, kc, 0:n], d_xT[:, kc, tok0:tok0 + n], W=[XK[kc]])
            rr = norm_rinv(xT, XK, n)
            for kc in range(8):
                stt(xnT[:, kc, 0:n], xT[:, kc, 0:n], dcol("gn32", kc), rr.t[:, 0:n], ALU.mult, ALU.mult, [XK[kc], "dv", rr.key], ["xnT"])
            rr.free()
            ALIAS_OK[0] = True

            def shifted(pb, pk, ucol, mu_ap, omm_ap):
                res = fal()
                r3 = v3(res.t[:, 0:n])
                p3 = v3(pb[:, 0:n])
                cU = S["U"][:, ucol, :].rearrange("p (a b) -> p a b", b=1)
                act(res.t[:, 0:n], pb[:, 0:n], AF.Identity, [pk, "dv"], [res.key], scale=omm_ap)
                stt(r3[:, :, 1:SL], p3[:, :, 0:SL - 1], mu_ap, r3[:, :, 1:SL], ALU.mult, ALU.add, [pk, "pv", res.key], [res.key])
                stt(r3[:, :, 0:1], cU, mu_ap, r3[:, :, 0:1], ALU.mult, ALU.add, [kU, "pv", res.key], [res.key])
                cp("act", cU, p3[:, :, SL - 1:SL], [pk], [kU])
                return res

            wr_base = wR.ptr
            wt, wk = wR.next()
            wv = wview(wt, 128)
            pb, pk = proj(wv, 0, n, wk)
            lo = shifted(pb, pk, 24, pv[:, PV_MULORA:PV_MULORA + 1], dv[:, DV_OMMLORA:DV_OMMLORA + 1])
            lorab = bal()
            act(lorab.t[0:64, 0:n], lo.t[0:64, 0:n], AF.Tanh, [lo.key], [lorab.key])
            cp("dve", lorab.t[64:128, 0:n], lo.t[64:128, 0:n], [lo.key], [lorab.key])
            lo.free()

            def rwkv_hp(j):
                while wR.ptr < wr_base + 1 + j:
                    yield
                wt, wk = wR.next()
                wv = wview(wt, 512)
                cs = slice(j * 128, (j + 1) * 128)
                pb, pk = proj(wv, 0, n, wk)
                r_s = shifted(pb, pk, j, pcol("mu_r", j), dcol("omm_r", j))
                pb, pk = proj(wv, 128, n, wk)
                k_s = shifted(pb, pk, 8 + j, pcol("mu_k", j), dcol("omm_k", j))
                pb, pk = proj(wv, 256, n, wk)
                v_s = shifted(pb, pk, 16 + j, pcol("mu_v", j), dcol("omm_v", j))
                pb, pk = proj(wv, 384, n, wk)
                sz = fal()
                act(sz.t[:, 0:n], pb[:, 0:n], AF.Silu, [pk], [sz.key])
                yield
                pb, pk = bank("M")
                mm(pb[:, 0:n], wlb[0:64, cs], lorab.t[0:64, 0:n], True, True, ["wlb", lorab.key], [pk])
                sig = fal()
                act(sig.t[:, 0:n], pb[:, 0:n], AF.Sigmoid, [pk, "pv"], [sig.key], bias=pcol("w0", j), scale=1.0)
                pb, pk = bank("M")
                mm(pb[:, 0:n], wlb[64:128, cs], lorab.t[64:128, 0:n], True, True, ["wlb", lorab.key], [pk])
                alpha = fal()
                act(alpha.t[:, 0:n], pb[:, 0:n], AF.Sigmoid, [pk, "pv"], [alpha.key], bias=pcol("a0", j), scale=1.0)
                yield
                cum = fal()
                scan(cum.t[:, 0:n], segmask[C][:, 0:n], sig.t[:, 0:n], [sig.key, "segm%d" % C], [cum.key])
                cumex = fal()
                tt("pool", cumex.t[:, 0:n], cum.t[:, 0:n], sig.t[:, 0:n], ALU.subtract, [cum.key, sig.key], [cumex.key])
                sig.free()
                dlt = fal()
                c3 = cum.t[:, 0:n].rearrange("p (a b) -> p a b", b=C)
                tt("pool", dlt.t[:, 0:n].rearrange("p (a b) -> p a b", b=C), c3[:, :, C - 1:C].broadcast_to([128, NCH, C]), c3, ALU.subtract, [cum.key], [dlt.key])
                Epos = fal(); Eneg = fal()
                act(Epos.t[:, 0:n], cum.t[:, 0:n], AF.Exp, [cum.key], [Epos.key], scale=-C0)
                act(Eneg.t[:, 0:n], cum.t[:, 0:n], AF.Exp, [cum.key], [Eneg.key], scale=C0)
                act(cumex.t[:, 0:n], cumex.t[:, 0:n], AF.Exp, [cumex.key], [cumex.key], scale=-C0)
                act(dlt.t[:, 0:n], dlt.t[:, 0:n], AF.Exp, [dlt.key], [dlt.key], scale=-C0)
                cum.free()
                Eex, Ehat = cumex, dlt
                yield
                kksq = bal()
                act(kksq.t[:, 0:n], k_s.t[:, 0:n], AF.Square, [k_s.key, "pv"], [kksq.key], scale=pcol("k_k", j))
                pb, pk = bank("M")
                mm(pb[:, 0:n], bd1b[:], kksq.t[:, 0:n], True, True, ["bd1b", kksq.key], [pk])
                kksq.free()
                rn = fal()
                rsqrt_from(rn.t[:, 0:n], pb[:, 0:n], 0, [pk], [rn.key])
                kk = fal()
                stt(kk.t[:, 0:n], k_s.t[:, 0:n], pcol("k_k", j), rn.t[:, 0:n], ALU.mult, ALU.mult, [k_s.key, "pv", rn.key], [kk.key])
                rn.free()
                yield
                kmod = fal()
                act(kmod.t[:, 0:n], alpha.t[:, 0:n], AF.Identity, [alpha.key, "pv", "dv"], [kmod.key], bias=dcol("omka", j), scale=pcol("k_a", j))
                tt("pool", kmod.t[:, 0:n], kmod.t[:, 0:n], k_s.t[:, 0:n], ALU.mult, [kmod.key, k_s.key], [kmod.key])
                k_s.free()
                bvec = fal()
                tt("dve", bvec.t[:, 0:n], kk.t[:, 0:n], alpha.t[:, 0:n], ALU.mult, [kk.key, alpha.key], [bvec.key])
                alpha.free()
                yield
                aT = bal(); btT = bal(); bhT = bal(); ktT = bal(); khT = bal(); rtT = bal(); vb = bal()
                stt(aT.t[:, 0:n], kk.t[:, 0:n], -1.0, Eex.t[:, 0:n], ALU.mult, ALU.mult, [kk.key, Eex.key], [aT.key])
                tt("dve", btT.t[:, 0:n], bvec.t[:, 0:n], Eneg.t[:, 0:n], ALU.mult, [bvec.key, Eneg.key], [btT.key])
                tt("dve", bhT.t[:, 0:n], bvec.t[:, 0:n], Ehat.t[:, 0:n], ALU.mult, [bvec.key, Ehat.key], [bhT.key])
                tt("pool", ktT.t[:, 0:n], kmod.t[:, 0:n], Eneg.t[:, 0:n], ALU.mult, [kmod.key, Eneg.key], [ktT.key])
                tt("pool", khT.t[:, 0:n], kmod.t[:, 0:n], Ehat.t[:, 0:n], ALU.mult, [kmod.key, Ehat.key], [khT.key])
                tt("dve", rtT.t[:, 0:n], r_s.t[:, 0:n], Epos.t[:, 0:n], ALU.mult, [r_s.key, Epos.key], [rtT.key])
                cp("pool", vb.t[:, 0:n], v_s.t[:, 0:n], [v_s.key], [vb.key])
                kk.free(); bvec.free(); Eex.free(); Ehat.free(); Eneg.free()
                yield
                rk = bal()
                stt(rk.t[:, 0:n], r_s.t[:, 0:n], pcol("r_k", j), kmod.t[:, 0:n], ALU.mult, ALU.mult, [r_s.key, "pv", kmod.key], [rk.key])
                r_s.free(); kmod.free()
                pb, pk = bank("M")
                mm(pb[:, 0:n], bd1b[:], rk.t[:, 0:n], True, True, ["bd1b", rk.key], [pk])
                bonus = fal()
                tt("dve", bonus.t[:, 0:n], pb[:, 0:n], v_s.t[:, 0:n], ALU.mult, [pk, v_s.key], [bonus.key])
                rk.free(); v_s.free()
                yield

                if sample:
                    RWt, kRW = rwj[j % 2], "rwj%d" % (j % 2)
                    dma("sp", RWt[:], d_strw[:, j], W=[kRW])
                else:
                    RWt, kRW = st["p"]["RW"][:, j], "cRW_p"
                osb = fal()
                nbatch = NCH // 8
                msl, msu, mue, meq = ((masks[(nm, C)], "mk_%s_%d" % (nm, C)) for nm in ("sl", "su", "ue", "eq"))
                evc = [0]

                def evac_eng():
                    evc[0] += 1
                    return "act" if evc[0] % 2 else "dve"
                for b in range(nbatch):
                    def cols(cc):
                        c = b * 8 + cc
                        return slice(c * C, (c + 1) * C)
                    toks = {}
                    for nm, src in (("A", aT), ("Bh", bhT), ("Kh", khT), ("V", vb)):
                        tk = bal()
                        for hh in range(2):
                            pt, ptk = psT[hh], "psT%d" % hh
                            for cc in range(8):
                                tr(pt[64 * hh:64 * hh + C, cc * 64:(cc + 1) * 64], src.t[64 * hh:64 * hh + 64, cols(cc)],
                                   ident[64 * hh:64 * hh + 64, 64 * hh:64 * hh + 64], [src.key, "ident"], [ptk])
                            cp(evac_eng(), tk.t[64 * hh:64 * hh + C, 0:512], pt[64 * hh:64 * hh + C, 0:512], [ptk], [tk.key])
                        toks[nm] = tk
                        yield
                    if b == nbatch - 1:
                        bhT.free(); khT.free(); vb.free()

                    def amat(L, R, mask, pool_="A"):
                        pp = bankpair()
                        o = bal()
                        for hh in range(2):
                            pa, pak = pp[hh]
                            hs_ = slice(64 * hh, 64 * hh + 64)
                            for cc in range(8):
                                mm(pa[64 * hh:64 * hh + C, cc * C:(cc + 1) * C], L.t[hs_, cols(cc)], R.t[hs_, cols(cc)],
                                   True, True, [L.key, R.key], [pak])
                            tt("dve", o.t[hs_, 0:8 * C].rearrange("p (a b) -> p a b", b=C), pa[hs_, 0:8 * C].rearrange("p (a b) -> p a b", b=C),
                               mask[0][hs_].broadcast_to([64, 8, C]), ALU.mult, [pak, mask[1]], [o.key])
                        return o

                    def mmat(L, R, lw, rw, addto=None):
                        pp = bankpair()
                        o = bal()
                        for hh in range(2):
                            pa, pak = pp[hh]
                            hs_ = slice(64 * hh, 64 * hh + 64)
                            for cc in range(8):
                                mm(pa[64 * hh:64 * hh + lw, cc * rw:(cc + 1) * rw],
                                   L.t[64 * hh:64 * hh + C, cc * lw:(cc + 1) * lw], R.t[64 * hh:64 * hh + C, cc * rw:(cc + 1) * rw],
                                   True, True, [L.key, R.key], [pak])
                            if addto is None:
                                cp(evac_eng(), o.t[hs_, 0:8 * rw], pa[hs_, 0:8 * rw], [pak], [o.key])
                            else:
                                tt("dve", o.t[hs_, 0:8 * rw], pa[hs_, 0:8 * rw], addto.t[hs_, 0:8 * rw], ALU.add, [pak, addto.key], [o.key])
                        return o
                    def multi(specs):
                        assert len(specs) >= 2
                        banks = [bank("A") for _ in specs]
                        outs = [bal() for _ in specs]
                        for hh in range(2):
                            hs_ = slice(64 * hh, 64 * hh + 64)
                            for sp, (pa, pak) in zip(specs, banks):
                                if sp[0] == "a":
                                    _, L, R, mask = sp
                                    for cc in range(8):
                                        mm(pa[64 * hh:64 * hh + C, cc * C:(cc + 1) * C], L.t[hs_, cols(cc)], R.t[hs_, cols(cc)],
                                           True, True, [L.key, R.key], [pak])
                                else:
                                    _, L, R, lw, rw, addto = sp
                                    for cc in range(8):
                                        mm(pa[64 * hh:64 * hh + lw, cc * rw:(cc + 1) * rw],
                                           L.t[64 * hh:64 * hh + C, cc * lw:(cc + 1) * lw], R.t[64 * hh:64 * hh + C, cc * rw:(cc + 1) * rw],
                                           True, True, [L.key, R.key], [pak])
                        for sp, (pa, pak), o in zip(specs, banks, outs):
                            nrow = C if sp[0] == "a" else sp[3]
                            rsl = [slice(0, 128)] if nrow == 64 else [slice(0, nrow), slice(64, 64 + nrow)]
                            for r_ in rsl:
                                np_ = r_.stop - r_.start
                                if sp[0] == "a":
                                    mask = sp[3]
                                    tt("dve", o.t[r_, 0:8 * C].rearrange("p (a b) -> p a b", b=C), pa[r_, 0:8 * C].rearrange("p (a b) -> p a b", b=C),
                                       mask[0][r_].broadcast_to([np_, 8, C]), ALU.mult, [pak, mask[1]], [o.key])
                                else:
                                    rw, addto = sp[4], sp[5]
                                    if addto is None:
                                        cp(evac_eng(), o.t[r_, 0:8 * rw], pa[r_, 0:8 * rw], [pak], [o.key])
                                    else:
                                        tt("dve", o.t[r_, 0:8 * rw], pa[r_, 0:8 * rw], addto.t[r_, 0:8 * rw], ALU.add, [pak, addto.key], [o.key])
                        return outs

                    Pc, PTc, AakT = multi([("a", aT, btT, msl), ("a", btT, aT, msu), ("a", ktT, aT, msu)])
                    yield
                    ArbT, ArkT = multi([("a", btT, rtT, mue), ("a", ktT, rtT, mue)])
                    yield
                    if b == nbatch - 1:
                        aT.free(); btT.free(); ktT.free()
                    Q = bal()
                    for r_ in ([slice(0, 128)] if C == 64 else [slice(0, C), slice(64, 64 + C)]):
                        np_ = r_.stop - r_.start
                        tt("pool", Q.t[r_, 0:8 * C].rearrange("p (a b) -> p a b", b=C), PTc.t[r_, 0:8 * C].rearrange("p (a b) -> p a b", b=C),
                           meq[0][r_].broadcast_to([np_, 8, C]), ALU.add, [PTc.key, meq[1]], [Q.key])
                    lvl = 2
                    pendP = None
                    while lvl < C:
                        last = (lvl * 2 >= C)
                        specs = [("m", PTc, Pc, C, C, None)]
                        if not last:
                            specs.append(("m", Pc, PTc, C, C, None))
                        if pendP is not None:
                            specs.append(("m", pendP, Q, C, C, Q))
                        if len(specs) == 1:
                            specs.append(("m", Pc, PTc, C, C, None))
                        res = multi(specs)
                        yield
                        Pn = res[0]
                        PTn = res[1] if not last else None
                        extra_unused = None
                        if last and pendP is None:
                            extra_unused = res[1]
                        if pendP is not None:
                            Qn = res[-1]
                            Q.free(); pendP.free()
                            Q = Qn
                        elif extra_unused is not None:
                            extra_unused.free()
                        PTc.free()
                        pendP_old = Pc
                        if pendP is None:
                            Pc.free()
                        pendP = Pn
                        Pc, PTc = Pn, PTn
                        lvl *= 2
                    res = multi([("m", pendP, Q, C, C, Q), ("m", AakT, toks["V"], C, 64, None)])
                    yield
                    Q.free(); pendP.free(); AakT.free()
                    Q, Xv = res[0], res[1]
                    if PTc is not None:
                        PTc.free()
                    WT = mmat(toks["A"], Q, 64, C)
                    toks["A"].free()
                    yield
                    if sample:
                        Hall = RWt[:, b * 8:(b + 1) * 8, :]
                        Hb = bal(); Ut = bal()
                        cp("act", Hb.t[:, 0:512].rearrange("p (a b) -> p a b", b=64), Hall, [kRW], [Hb.key])
                        for hh in range(2):
                            pu, puk = psA[hh], "psA%d" % hh
                            rs = slice(64 * hh, 64 * hh + C)
                            fs = slice(64 * hh, 64 * hh + 64)
                            for cc in range(8):
                                mm(pu[rs, cc * 64:(cc + 1) * 64], Q.t[rs, cc * C:(cc + 1) * C], Xv.t[rs, cc * 64:(cc + 1) * 64], True, False, [Q.key, Xv.key], [puk])
                                mm(pu[rs, cc * 64:(cc + 1) * 64], WT.t[fs, cc * C:(cc + 1) * C], Hb.t[fs, cc * 64:(cc + 1) * 64], False, True, [WT.key, Hb.key], [puk])
                        for hh in range(2):
                            fs = slice(64 * hh, 64 * hh + 64)
                            rs = slice(64 * hh, 64 * hh + C)
                            cp("dve" if hh == 0 else "act", Ut.t[rs, 0:512], psA[hh][rs, 0:512], ["psA%d" % hh], [Ut.key])
                        yield
                        for hh in range(2):
                            ph, phk = psA[2 + hh], "psA%d" % (2 + hh)
                            rs = slice(64 * hh, 64 * hh + C)
                            fs = slice(64 * hh, 64 * hh + 64)
                            for cc in range(8):
                                mm(ph[fs, cc * 64:(cc + 1) * 64], toks["Kh"].t[rs, cc * 64:(cc + 1) * 64], toks["V"].t[rs, cc * 64:(cc + 1) * 64], True, False,
                                   [toks["Kh"].key, toks["V"].key], [phk])
                                mm(ph[fs, cc * 64:(cc + 1) * 64], toks["Bh"].t[rs, cc * 64:(cc + 1) * 64], Ut.t[rs, cc * 64:(cc + 1) * 64], False, True,
                                   [toks["Bh"].key, Ut.key], [phk])
                        for hh in range(2):
                            pO, pOk = psA[hh], "psA%d" % hh
                            rs = slice(64 * hh, 64 * hh + C)
                            fs = slice(64 * hh, 64 * hh + 64)
                            for cc in range(8):
                                c = b * 8 + cc
                                mm(pO[fs, cc * C:(cc + 1) * C], Hb.t[fs, cc * 64:(cc + 1) * 64], rtT.t[fs, c * C:(c + 1) * C], True, False, [Hb.key, rtT.key], [pOk])
                                mm(pO[fs, cc * C:(cc + 1) * C], toks["V"].t[rs, cc * 64:(cc + 1) * 64], ArkT.t[rs, cc * C:(cc + 1) * C], False, False,
                                   [toks["V"].key, ArkT.key], [pOk])
                                mm(pO[fs, cc * C:(cc + 1) * C], Ut.t[rs, cc * 64:(cc + 1) * 64], ArbT.t[rs, cc * C:(cc + 1) * C], False, True, [Ut.key, ArbT.key], [pOk])
                        gCall = Epos.t[:, 0:n].rearrange("p (a b) -> p a b", b=C)[:, b * 8:(b + 1) * 8, C - 1:C].broadcast_to([128, 8, 64])
                        tt("pool", Hall, Hall, gCall, ALU.mult, [kRW, Epos.key], [kRW])
                        for hh in range(2):
                            fs = slice(64 * hh, 64 * hh + 64)
                            tt("dve", Hall[fs], Hall[fs], psA[2 + hh][fs, 0:512].rearrange("p (a b) -> p a b", b=64), ALU.add, [kRW, "psA%d" % (2 + hh)], [kRW])
                            cp("act", osb.t[fs, b * 8 * C:(b + 1) * 8 * C], psA[hh][fs, 0:8 * C], ["psA%d" % hh], [osb.key])
                        Hb.free(); Ut.free()
                        yield
                    Hb_next = None
                    for cc in (range(8) if not sample else ()):
                        c = b * 8 + cc
                        seg = c if sample else 0
                        Hf = RWt[:, seg, :]
                        Ut = bal()
                        if sample or Hb_next is None:
                            Hb = bal()
                            cp("act", Hb.t[:, 0:64], Hf, [kRW], [Hb.key])
                        else:
                            Hb = Hb_next
                        for hh in range(2):
                            pu, puk = psA[hh], "psA%d" % hh
                            rs = slice(64 * hh, 64 * hh + C)
                            fs = slice(64 * hh, 64 * hh + 64)
                            mm(pu[rs, 0:64], Q.t[rs, cc * C:(cc + 1) * C], Xv.t[rs, cc * 64:(cc + 1) * 64], True, False, [Q.key, Xv.key], [puk])
                            mm(pu[rs, 0:64], WT.t[fs, cc * C:(cc + 1) * C], Hb.t[fs, 0:64], False, True, [WT.key, Hb.key], [puk])
                        for hh in range(2):
                            fs = slice(64 * hh, 64 * hh + 64)
                            cp("dve" if hh == 0 else "act", Ut.t[fs, 0:64], psA[hh][fs, 0:64], ["psA%d" % hh], [Ut.key])
                        for hh in range(2):
                            ph, phk = psA[2 + hh], "psA%d" % (2 + hh)
                            rs = slice(64 * hh, 64 * hh + C)
                            fs = slice(64 * hh, 64 * hh + 64)
                            mm(ph[fs, 0:64], toks["Kh"].t[rs, cc * 64:(cc + 1) * 64], toks["V"].t[rs, cc * 64:(cc + 1) * 64], True, False,
                               [toks["Kh"].key, toks["V"].key], [phk])
                            mm(ph[fs, 0:64], toks["Bh"].t[rs, cc * 64:(cc + 1) * 64], Ut.t[rs, 0:64], False, True, [toks["Bh"].key, Ut.key], [phk])
                        for hh in range(2):
                            pO, pOk = psA[2 + hh], "psA%d" % (2 + hh)
                            rs = slice(64 * hh, 64 * hh + C)
                            fs = slice(64 * hh, 64 * hh + 64)
                            mm(pO[fs, 64:64 + C], Hb.t[fs, 0:64], rtT.t[fs, c * C:(c + 1) * C], True, False, [Hb.key, rtT.key], [pOk])
                            mm(pO[fs, 64:64 + C], toks["V"].t[rs, cc * 64:(cc + 1) * 64], ArkT.t[rs, cc * C:(cc + 1) * C], False, False,
                               [toks["V"].key, ArkT.key], [pOk])
                            mm(pO[fs, 64:64 + C], Ut.t[rs, 0:64], ArbT.t[rs, cc * C:(cc + 1) * C], False, True, [Ut.key, ArbT.key], [pOk])
                        gC = Epos.t[:, c * C + C - 1:c * C + C]
                        Hb_next = None
                        if (not sample) and cc < 7:
                            Hb_next = bal()
                            for hh in range(2):
                                fs = slice(64 * hh, 64 * hh + 64)
                                stt(Hb_next.t[fs, 0:64], Hf[fs], gC[fs], psA[2 + hh][fs, 0:64], ALU.mult, ALU.add, [kRW, Epos.key, "psA%d" % (2 + hh)], [Hb_next.key])
                        for hh in range(2):
                            fs = slice(64 * hh, 64 * hh + 64)
                            stt(Hf[fs], Hf[fs], gC[fs], psA[2 + hh][fs, 0:64], ALU.mult, ALU.add, [kRW, Epos.key, "psA%d" % (2 + hh)], [kRW])
                        for hh in range(2):
                            fs = slice(64 * hh, 64 * hh + 64)
                            cp("act", osb.t[fs, c * C:(c + 1) * C], psA[2 + hh][fs, 64:64 + C], ["psA%d" % (2 + hh)], [osb.key])
                        Hb.free(); Ut.free()
                        yield
                    for t_ in (Q, Xv, WT, ArbT, ArkT, toks["Bh"], toks["Kh"], toks["V"]):
                        t_.free()
                for t_ in (rtT, Epos):
                    t_.free()
                if sample:
                    dma("sp", o_rws[:, j], RWt[:], R=[kRW])
                pb, pk = bank("M")
                mm(pb[:, 0:n], bd64[:], osb.t[:, 0:n], True, True, ["bd64", osb.key], [pk])
                dd = fal()
                tt("dve", dd.t[:, 0:n], osb.t[:, 0:n], pb[:, 0:n], ALU.subtract, [osb.key, pk], [dd.key])
                yield
                dsq = bal()
                act(dsq.t[:, 0:n], dd.t[:, 0:n], AF.Square, [dd.key], [dsq.key])
                pb, pk = bank("M")
                mm(pb[:, 0:n], bd64b[:], dsq.t[:, 0:n], True, True, ["bd64b", dsq.key], [pk])
                dsq.free()
                rstd = fal()
                rsqrt_from(rstd.t[:, 0:n], pb[:, 0:n], 1, [pk], [rstd.key])
                yield
                stt(dd.t[:, 0:n], dd.t[:, 0:n], pcol("gn_g", j), rstd.t[:, 0:n], ALU.mult, ALU.mult, [dd.key, "pv", rstd.key], [dd.key])
                stt(dd.t[:, 0:n], dd.t[:, 0:n], pcol("gn_b", j), bonus.t[:, 0:n], ALU.add, ALU.add, [dd.key, "pv", bonus.key], [dd.key])
                tt("dve", oT[0][:, j, 0:n], dd.t[:, 0:n], sz.t[:, 0:n], ALU.mult, [dd.key, sz.key], ["oT0"])
                for t_ in (osb, dd, rstd, bonus, sz):
                    t_.free()
                yield

            def R_stream(par, delay):
                for _ in range(delay):
                    yield
                for j_ in range(par, 8, 2):
                    yield from rwkv_hp(j_)

            def lru_chunk(j):
                wt, wk = wL.next()
                wv = wview(wt, 256)
                pb, pk = proj(wv, 0, n, wk)
                xb = fal(True)
                XB3 = xb.t[:, 0:NSEG * (SL + 3)].rearrange("p (a b) -> p a b", b=SL + 3)
                act(XB3[:, :, 3:SL + 3], v3(pb[:, 0:n]), AF.Copy, [pk], [xb.key])
                cp("act", XB3[:, :, 0:3], S["CV"][:, j, :, :], [kCV], [xb.key])
                pb, pk = proj(wv, 128, n, wk)
                szb = fal()
                act(szb.t[:, 0:n], pb[:, 0:n], AF.Silu, [pk], [szb.key])
                yield
                xc = fal()
                ts("dve", v3(xc.t[:, 0:n]), XB3[:, :, 0:SL], pcol("cw0", j), ALU.mult, [xb.key, "pv"], [xc.key], s2=pcol("conv_b", j), op1=ALU.add)
                for q in (1, 2, 3):
                    stt(v3(xc.t[:, 0:n]), XB3[:, :, q:q + SL], pcol("cw%d" % q, j), v3(xc.t[:, 0:n]), ALU.mult, ALU.add, [xb.key, "pv", xc.key], [xc.key])
                cp("pool", S["CV"][:, j, :, :], XB3[:, :, SL:SL + 3], [xb.key], [kCV])
                xb.free()
                xcb = bal()
                cp("pool", xcb.t[:, 0:n], xc.t[:, 0:n], [xc.key], [xcb.key])
                yield
                pb, pk = bank("M")
                mm(pb[:, 0:n], wrg[:, j, 0, :], xcb.t[:, 0:n], True, True, ["wrg", xcb.key], [pk])
                gr = fal()
                act(gr.t[:, 0:n], pb[:, 0:n], AF.Sigmoid, [pk, "pv"], [gr.key], bias=pcol("b_rg_a", j), scale=1.0)
                pb, pk = bank("M")
                mm(pb[:, 0:n], wrg[:, j, 1, :], xcb.t[:, 0:n], True, True, ["wrg", xcb.key], [pk])
                gi = fal()
                act(gi.t[:, 0:n], pb[:, 0:n], AF.Sigmoid, [pk, "pv"], [gi.key], bias=pcol("b_rg_x", j), scale=1.0)
                xcb.free()
                yield
                aa = fal(); a2 = fal()
                act(aa.t[:, 0:n], gr.t[:, 0:n], AF.Exp, [gr.key, "dv"], [aa.key], scale=dcol("clam", j))
                act(a2.t[:, 0:n], gr.t[:, 0:n], AF.Exp, [gr.key, "dv"], [a2.key], scale=dcol("clam2", j))
                ts("dve", a2.t[:, 0:n], a2.t[:, 0:n], -1.0, ALU.mult, [a2.key], [a2.key], s2=1.0, op1=ALU.add)
                act(a2.t[:, 0:n], a2.t[:, 0:n], AF.Sqrt, [a2.key], [a2.key])
                tt("dve", gi.t[:, 0:n], gi.t[:, 0:n], a2.t[:, 0:n], ALU.mult, [gi.key, a2.key], [gi.key])
                tt("dve", gi.t[:, 0:n], gi.t[:, 0:n], xc.t[:, 0:n], ALU.mult, [gi.key, xc.key], [gi.key])
                xc.free(); gr.free(); a2.free()
                yield
                a3 = v3(aa.t[:, 0:n]); b3 = v3(gi.t[:, 0:n])
                tmp = fal()
                h0 = S["H"][:, j, :].rearrange("p (a b) -> p a b", b=1)
                tt("pool", tmp.t[:, 0:NSEG].rearrange("p (a b) -> p a b", b=1), a3[:, :, 0:1], h0, ALU.mult, [aa.key, kH], [tmp.key])
                tt("pool", b3[:, :, 0:1], b3[:, :, 0:1], tmp.t[:, 0:NSEG].rearrange("p (a b) -> p a b", b=1), ALU.add, [gi.key, tmp.key], [gi.key])
                memset("pool", a3[:, :, 0:1], 0.0, [aa.key])
                tmp.free()
                hs = fal()
                scan(hs.t[:, 0:n], aa.t[:, 0:n], gi.t[:, 0:n], [aa.key, gi.key], [hs.key])
                cp("pool", h0, v3(hs.t[:, 0:n])[:, :, SL - 1:SL], [hs.key], [kH])
                tt("dve", oT[1][:, j, 0:n], hs.t[:, 0:n], szb.t[:, 0:n], ALU.mult, [hs.key, szb.key], ["oT1"])
                for t_ in (aa, gi, hs, szb):
                    t_.free()
                yield

            def attn_head(hd):
                qT = bal(); szc = [fal(), fal()]
                for dc in range(2):
                    wt, wk = wL.next()
                    wv = wview(wt, 256)
                    pb, pk = proj(wv, 0, n, wk)
                    if dc == 0:
                        qd0 = qT
                        act(qd0.t[:, 0:n], pb[:, 0:n], AF.Identity, [pk], [qd0.key], scale=0.0625)
                    else:
                        qd1 = bal()
                        act(qd1.t[:, 0:n], pb[:, 0:n], AF.Identity, [pk], [qd1.key], scale=0.0625)
                    pb, pk = proj(wv, 128, n, wk)
                    act(szc[dc].t[:, 0:n], pb[:, 0:n], AF.Silu, [pk], [szc[dc].key])
                yield
                qd = [qd0, qd1]
                pTall = [bal(), bal()]
                if not sample:
                    groups = [(i * 128, 128, 0) for i in range(n // 128)]
                else:
                    groups = [(s_ * SLEN, SLEN, s_) for s_ in range(NSEQ)]
                ocr = fal() if sample else None
                for (t0, tq, sidx) in groups:
                    if sample:
                        kb = ksrot[0] % NKS
                        ksrot[0] += 1
                        dma("pool", KTs[kb][:], d_kTs[sidx, :, hd * 2:hd * 2 + 2, :], W=["KTs%d" % kb])
                        dma("pool", VTs[kb][:], d_vs[sidx, :, :, hd * 256:(hd + 1) * 256], W=["VTs%d" % kb])
                        ktile, kkey, vtile, vkey = KTs[kb], "KTs%d" % kb, VTs[kb], "VTs%d" % kb
                        kofs, vofs = 0, 0
                    else:
                        ktile, kkey, vtile, vkey = KT[0], "KT0", VT[0], "VT0"
                        kofs, vofs = hd * 2, hd * 256
                    ai = rot["A3"] % 3
                    rot["A3"] += 1
                    pa, pak = psA[ai], "psA%d" % ai
                    for dc in range(2):
                        mm(pa[0:tq, 0:256], qd[dc].t[:, t0:t0 + tq], ktile[:, kofs + dc, :], dc == 0, dc == 1, [qd[dc].key, kkey], [pak])
                    mx = fal()
                    P.add("dve", lambda h, mx=mx, pa=pa, tq=tq: h.tensor_reduce(out=mx.t[0:tq, 0:1], in_=pa[0:tq, 0:256], axis=AX.X, op=ALU.max), reads=[pak], writes=[mx.key])
                    ts("dve", mx.t[0:tq, 1:2], mx.t[0:tq, 0:1], -1.0, ALU.mult, [mx.key], [mx.key])
                    pe_ = bal()
                    act(pe_.t[0:tq, 0:256], pa[0:tq, 0:256], AF.Exp, [pak, mx.key], [pe_.key, mx.key], bias=mx.t[0:tq, 1:2], scale=1.0, accum=mx.t[0:tq, 2:3])
                    P.add("dve", lambda h, mx=mx, tq=tq: h.reciprocal(out=mx.t[0:tq, 3:4], in_=mx.t[0:tq, 2:3]), reads=[mx.key], writes=[mx.key])
                    ts("dve", pe_.t[0:tq, 0:256], pe_.t[0:tq, 0:256], mx.t[0:tq, 3:4], ALU.mult, [pe_.key, mx.key], [pe_.key])
                    pt, ptk = bank("T")
                    for mh in range(2):
                        tr(pt[:, mh * 128:mh * 128 + tq], pe_.t[0:tq, mh * 128:(mh + 1) * 128], ident[0:tq, 0:tq], [pe_.key, "ident"], [ptk])
                    for mh in range(2):
                        cp("act" if mh == 0 else "dve", pTall[mh].t[:, t0:t0 + tq], pt[:, mh * 128:mh * 128 + tq], [ptk], [pTall[mh].key])
                    mx.free(); pe_.free()
                    if sample:
                        pb, pk = bank("M")
                        for dc in range(2):
                            for mh in range(2):
                                mm(pb[:, dc * tq:(dc + 1) * tq], vtile[:, mh, vofs + dc * 128:vofs + (dc + 1) * 128],
                                   pTall[mh].t[:, t0:t0 + tq], mh == 0, mh == 1, [vkey, pTall[mh].key], [pk])
                        act(ocr.t[:, 0:256].rearrange("p (a b) -> p a b", a=2)[:, :, t0:t0 + tq], pb[:, 0:2 * tq].rearrange("p (a b) -> p a b", a=2),
                            AF.Copy, [pk], [ocr.key])
                    yield
                for dc in range(2):
                    if not sample:
                        pb, pk = bank("M")
                        for mh in range(2):
                            mm(pb[:, 0:n], VT[0][:, mh, hd * 256 + dc * 128:hd * 256 + (dc + 1) * 128], pTall[mh].t[:, 0:n], mh == 0, mh == 1,
                               ["VT0", pTall[mh].key], [pk])
                        src_ps, src_k = pb[:, 0:n], pk
                    else:
                        src_ps, src_k = ocr.t[:, dc * 128:dc * 128 + n], ocr.key
                    tt("dve", oT[2][:, hd * 2 + dc, 0:n], src_ps, szc[dc].t[:, 0:n], ALU.mult, [src_k, szc[dc].key], ["oT2"])
                for t_ in (qd0, qd1, szc[0], szc[1], pTall[0], pTall[1]):
                    t_.free()
                if sample:
                    ocr.free()
                yield

            def L_stream():
                for j_ in range(8):
                    yield from lru_chunk(j_)
                for hd_ in range(4):
                    yield from attn_head(hd_)

            gens = [R_stream(0, 0), R_stream(1, 16), L_stream()]
            ratios = [1, 1, 1]
            while any(g is not None for g in gens):
                for gi_, g in enumerate(gens):
                    if g is None:
                        continue
                    for _ in range(ratios[gi_]):
                        try:
                            next(g)
                        except StopIteration:
                            gens[gi_] = None
                            break
            lorab.free()
            ALIAS_OK[0] = False
            assert len(freeFA) == NFA
            mlist[0] = MWIDE
            for kc in range(8):
                dma("sp", xT[:, kc, 0:n], d_xT[:, kc, tok0:tok0 + n], W=[XK[kc]])
            for m in range(8):
                wt, wk = wM.next()
                wvg = wview(wt, 384)
                wt2, wk2 = wM.next()
                wvb = wt2[:, 0:3 * 8 * 128].rearrange("p (b k c) -> p b k c", b=3, k=8)
                acc = fal()
                for br in range(3):
                    pb, pk = proj(wvg, br * 128, n, wk)
                    g = fal()
                    act(g.t[:, 0:n], pb[:, 0:n], AF.Sigmoid, [pk], [g.key])
                    pb2, pk2 = bank("M")
                    for kc in range(8):
                        mm(pb2[:, 0:n], wvb[:, br, kc, :], oT[br][:, kc, 0:n], kc == 0, kc == 7, [wk2, "oT%d" % br], [pk2])
                    if br == 0:
                        tt("dve", acc.t[:, 0:n], pb2[:, 0:n], g.t[:, 0:n], ALU.mult, [pk2, g.key], [acc.key])
                    else:
                        tt("dve", g.t[:, 0:n], pb2[:, 0:n], g.t[:, 0:n], ALU.mult, [pk2, g.key], [g.key])
                        if br == 1:
                            tt("pool", acc.t[:, 0:n], acc.t[:, 0:n], g.t[:, 0:n], ALU.add, [acc.key, g.key], [acc.key])
                        else:
                            tt("dve", mgT[:, m, 0:n], acc.t[:, 0:n], g.t[:, 0:n], ALU.add, [acc.key, g.key], ["mgT"])
                    g.free()
                acc.free()

            for oc in range(8):
                wt, wk = wM.next()
                wvo = wview(wt, 128)
                pb, pk = bank("M")
                for kc in range(8):
                    mm(pb[:, 0:n], wvo[:, kc, :], mgT[:, kc, 0:n], kc == 0, kc == 7, [wk, "mgT"], [pk])
                tt("dve", xT[:, oc, 0:n], pb[:, 0:n], xT[:, oc, 0:n], ALU.add, [pk, XK[oc]], [XK[oc]])
            mlist[0] = MNARROW
            rf = norm_rinv(xT, XK, n)
            for oc in range(8):
                stt(xT[:, oc, 0:n], xT[:, oc, 0:n], dcol("gf32", oc), rf.t[:, 0:n], ALU.mult, ALU.mult, [XK[oc], "dv", rf.key], [XK[oc]])
                dma("sp", o_yT[:, oc, tok0:tok0 + n], xT[:, oc, 0:n], R=[XK[oc]])
            rf.free()

        if DEBUG:
            for b_ in range(3):
                d_ = nc.dram_tensor("dbg_o%d" % b_, [128, 8, NMAX], BF16, kind="ExternalOutput").ap()
                dma("sp", d_, oT[b_][:], R=["oT%d" % b_])
            d_ = nc.dram_tensor("dbg_mg", [128, 8, NMAX], BF16, kind="ExternalOutput").ap()
            dma("sp", d_, mgT[:], R=["mgT"])
        dma("sp", o_rwp, st["p"]["RW"][:], R=["cRW_p"])
        dma("sp", o_shp, st["p"]["U"][:], R=["cU_p"])
        dma("sp", o_lrup, st["p"]["H"][:], R=["cH_p"])
        dma("sp", o_cvp, st["p"]["CV"][:], R=["cCV_p"])
        dma("sp", o_shs, st["s"]["U"][:], R=["cU_s"])
        dma("sp", o_lrus, st["s"]["H"][:], R=["cH_s"])
        dma("sp", o_cvs, st["s"]["CV"][:], R=["cCV_s"])
        for w_ in (wR, wL, wM):
            assert w_.ptr == len(w_.list), (w_.ptr, len(w_.list))
        P.emit(es)
    return nc


_NC_CACHE = {}


def kernel(x_prompt, x_sample, mem_prompt, state_rwkv, state_shift, state_lru, state_conv,
           cache_mem_k, cache_mem_v, g_norm, w_in, mu_shift, w0, w_decay, a0, w_aaa, k_k, k_a, r_k,
           gn_g, gn_b, conv_w, conv_b, w_rg_a, b_rg_a, w_rg_x, b_rg_x, lru_lambda, g_mem, w_mem_kv,
           w_br_a, w_br_b, w_br_c, w_out, g_final):
    f = lambda a: np.asarray(a, np.float32)
    x_prompt, x_sample, mem_prompt = f(x_prompt), f(x_sample), f(mem_prompt)
    perm = _colperm()
    w_in_l = np.ascontiguousarray(f(w_in)[0][:, perm].reshape(8, 128, NCOL).transpose(1, 0, 2))
    w_kv_l = np.ascontiguousarray(f(w_mem_kv)[0].reshape(8, 128, 2048).transpose(1, 0, 2))
    wbr = np.stack([f(w_br_a)[0], f(w_br_b)[0], f(w_br_c)[0]], 0)
    w_br_l = np.ascontiguousarray(wbr.reshape(3, 8, 128, 8, 128).transpose(2, 3, 0, 1, 4))
    w_out_l = np.ascontiguousarray(f(w_out)[0].reshape(8, 128, 8, 128).transpose(1, 2, 0, 3))
    wl = np.ascontiguousarray(np.concatenate([f(w_decay)[0], f(w_aaa)[0]], 0))
    mu = f(mu_shift)[0]
    cw = f(conv_w)[0]
    vecs = {"g_norm": f(g_norm)[0], "mu_r": mu[0:1024], "mu_k": mu[1024:2048], "mu_v": mu[2048:3072], "w0": f(w0)[0], "a0": f(a0)[0],
            "k_k": f(k_k)[0], "k_a": f(k_a)[0], "r_k": f(r_k)[0].reshape(1024), "gn_g": f(gn_g)[0], "gn_b": f(gn_b)[0],
            "cw0": cw[0], "cw1": cw[1], "cw2": cw[2], "cw3": cw[3], "conv_b": f(conv_b)[0], "b_rg_a": f(b_rg_a)[0], "b_rg_x": f(b_rg_x)[0],
            "lam": f(lru_lambda)[0], "g_mem": f(g_mem)[0], "g_final": f(g_final)}
    pvec = np.concatenate([_fm(vecs[nm]) for nm in PV_NAMES] + [mu[3072:3200].reshape(128, 1)], 1).astype(np.float32)
    pvec = np.ascontiguousarray(pvec)
    key = "nc"
    if key not in _NC_CACHE:
        _NC_CACHE[key] = build_nc()
    nc = _NC_CACHE[key]
    in_maps = []
    for c in range(NCORE):
        xs = x_sample[c * NSEQ:(c + 1) * NSEQ].reshape(TS, 1024)
        xa = np.concatenate([x_prompt[c], xs], 0)
        xT = np.ascontiguousarray(xa.T.reshape(8, 128, NTOK).transpose(1, 0, 2))
        memT = np.ascontiguousarray(mem_prompt[c].T.reshape(8, 128, 256).transpose(1, 0, 2))
        sl = slice(c * NSEQ, (c + 1) * NSEQ)
        ck = f(cache_mem_k)[0, sl]
        kT_s = np.ascontiguousarray(ck.transpose(0, 2, 3, 1).reshape(NSEQ, 4, 2, 128, 256).transpose(0, 3, 1, 2, 4).reshape(NSEQ, 128, 8, 256))
        cv = f(cache_mem_v)[0, sl].reshape(NSEQ, 2, 128, 1024)
        v_s = np.ascontiguousarray(cv.transpose(0, 2, 1, 3))
        srw = f(state_rwkv)[0, sl]
        st_rwkv = np.ascontiguousarray(srw.reshape(NSEQ, 8, 2, 64, 64).transpose(2, 4, 1, 0, 3).reshape(128, 8, NSEQ, 64))
        st_shift = np.ascontiguousarray(f(state_shift)[0, sl].reshape(NSEQ, 25, 128).transpose(2, 1, 0))
        st_lru = np.ascontiguousarray(f(state_lru)[0, sl].reshape(NSEQ, 8, 128).transpose(2, 1, 0))
        st_conv = np.ascontiguousarray(f(state_conv)[0, sl].reshape(NSEQ, 3, 8, 128).transpose(3, 2, 0, 1))
        in_maps.append({"xT": xT, "memT": memT, "w_in_l": w_in_l, "w_kv_l": w_kv_l, "w_br_l": w_br_l, "w_out_l": w_out_l, "wl": wl,
                        "w_rg_a": np.ascontiguousarray(f(w_rg_a)[0]), "w_rg_x": np.ascontiguousarray(f(w_rg_x)[0]), "pvec": pvec,
                        "kT_s": kT_s, "v_s": v_s, "st_rwkv": st_rwkv, "st_shift": st_shift, "st_lru": st_lru, "st_conv": st_conv})
    res = run_bass_kernel_spmd(nc, in_maps, core_ids=list(range(NCORE)))
    R = res.results
    if DEBUG:
        for k_ in ("dbg_o0", "dbg_o1", "dbg_o2", "dbg_mg"):
            DBG_OUT[k_] = np.asarray(R[0][k_]).astype(np.float32)
    y_p = np.empty((8, TP, 1024), np.float32); y_s = np.empty((128, SLEN, 1024), np.float32)
    rw_p = np.empty((1, 8, 16, 64, 64), np.float32); sh_p = np.empty((1, 8, 3200), np.float32)
    lru_p = np.empty((1, 8, 1024), np.float32); cv_p = np.empty((1, 8, 3, 1024), np.float32)
    mk_p = np.empty((1, 8, 256, 4, 256), np.float32); mv_p = np.empty((1, 8, 256, 4, 256), np.float32)
    rw_s = np.empty((1, 128, 16, 64, 64), np.float32); sh_s = np.empty((1, 128, 3200), np.float32)
    lru_s = np.empty((1, 128, 1024), np.float32); cv_s = np.empty((1, 128, 3, 1024), np.float32)

    def unrw(a):
        ns = a.shape[2]
        return a.reshape(2, 64, 8, ns, 64).transpose(3, 2, 0, 4, 1).reshape(ns, 16, 64, 64)
    for c in range(NCORE):
        r = R[c]
        y = r["yT"].transpose(2, 1, 0).reshape(NTOK, 1024)
        y_p[c] = y[:TP]
        sl = slice(c * NSEQ, (c + 1) * NSEQ)
        y_s[sl] = y[TP:].reshape(NSEQ, SLEN, 1024)
        rw_p[0, c] = unrw(r["rwkv_p"])[0]
        sh_p[0, c] = r["shift_p"].transpose(2, 1, 0).reshape(3200)
        lru_p[0, c] = r["lru_p"].transpose(2, 1, 0).reshape(1024)
        cv_p[0, c] = r["conv_p"].transpose(2, 3, 1, 0).reshape(3, 1024)
        mk_p[0, c] = r["memk_p"].reshape(128, 4, 2, 256).transpose(3, 1, 2, 0).reshape(256, 4, 256)
        mv_p[0, c] = r["memv_p"].transpose(1, 0, 2).reshape(256, 4, 256)
        rw_s[0, sl] = unrw(r["rwkv_s"])
        sh_s[0, sl] = r["shift_s"].transpose(2, 1, 0).reshape(NSEQ, 3200)
        lru_s[0, sl] = r["lru_s"].transpose(2, 1, 0).reshape(NSEQ, 1024)
        cv_s[0, sl] = r["conv_s"].transpose(2, 3, 1, 0).reshape(NSEQ, 3, 1024)
    return (y_p, y_s, rw_p, sh_p, lru_p, cv_p, mk_p, mv_p, rw_s, sh_s, lru_s, cv_s)
```

```python
import numpy as np
from contextlib import ExitStack
import concourse.bass as bass
import concourse.mybir as mybir
from concourse.bass_utils import run_bass_kernel_spmd

F32 = mybir.dt.float32
BF16 = mybir.dt.bfloat16
AF = mybir.ActivationFunctionType
ALU = mybir.AluOpType
AX = mybir.AxisListType

NCORE = 8
TP = 2048
NSEQ = 16
SLEN = 8
TS = NSEQ * SLEN
NTOK = TP + TS
NCOL = 11392
C0 = float(np.exp(-0.5))
ENABLE_RWKV = True
DEBUG = False
DBG_OUT = {}


class Op:
    __slots__ = ("eng", "fn", "deps", "dma", "sem", "semval", "sig", "sigidx", "seq", "gidx", "t_end")


class Prog:
    COMPUTE = ("pe", "act", "dve", "pool")

    def __init__(self, nc, n_dma_sems=14):
        self.nc = nc
        self.ops = []
        self.lastw = {}
        self.rd_eng = {}
        self.rd_dma = {}
        self.seq = {e: 0 for e in ("pe", "act", "dve", "pool", "sp")}
        self.n_dma_sems = n_dma_sems
        self.dma_count = {}
        self.eng_free = {e: 0.0 for e in ("pe", "act", "dve", "pool", "sp")}
        self.pe_last = {}
        self.mark_end = 0.0

    def add(self, eng, fn, reads=(), writes=(), dma=False, cost=0.3, rowtile=None):
        op = Op()
        op.eng = eng; op.fn = fn; op.dma = dma; op.sig = False; op.sigidx = 0
        op.sem = None; op.semval = 0
        op.seq = self.seq[eng]; self.seq[eng] += 1
        op.gidx = len(self.ops)
        writes = list(writes) + [k for k in reads if isinstance(k, str) and k.startswith("ps") and k not in writes]
        deps = {}

        def dep(p, raw):
            if p is None or p is op:
                return
            if (not dma) and (not p.dma) and p.eng == eng:
                if eng == "pe":
                    return
            deps[p.gidx] = p
        for k in reads:
            dep(self.lastw.get(k), True)
        for k in writes:
            dep(self.lastw.get(k), False)
            for r in self.rd_eng.get(k, {}).values():
                dep(r, False)
            for r in self.rd_dma.get(k, ()):
                dep(r, False)
        if eng == "pe" and rowtile is not None:
            for k in writes:
                if isinstance(k, str) and k.startswith("ps"):
                    last = self.pe_last.get(k)
                    if last is not None and (last[1][0] + last[1][1] <= rowtile[0] or rowtile[0] + rowtile[1] <= last[1][0]) and op.seq - last[0].seq <= 12:
                        deps[last[0].gidx] = last[0]
                    self.pe_last[k] = (op, rowtile)
        op.deps = list(deps.values())
        for p in op.deps:
            if not p.dma:
                p.sig = True
        t0_ = self.eng_free[eng]
        for p in op.deps:
            t0_ = max(t0_, p.t_end + 0.3)
        if dma:
            self.eng_free[eng] = t0_ + 0.1
            op.t_end = t0_ + cost
        else:
            op.t_end = t0_ + cost
            self.eng_free[eng] = op.t_end
        self.mark_end = max(self.mark_end, op.t_end)
        wset = set(writes)
        for k in writes:
            self.lastw[k] = op
            self.rd_eng[k] = {}
            self.rd_dma[k] = []
        for k in reads:
            if k in wset:
                continue
            if dma:
                self.rd_dma.setdefault(k, []).append(op)
            else:
                self.rd_eng.setdefault(k, {})[eng] = op
        if dma:
            i = self.dma_count.get(eng, 0)
            self.dma_count[eng] = i + 1
            op.sem = (eng, i % self.n_dma_sems)
            op.semval = 16 * (i // self.n_dma_sems + 1)
        self.ops.append(op)
        return op

    def emit(self, es):
        nc = self.nc
        esem = {e: es.enter_context(nc.semaphore("s_" + e)) for e in self.COMPUTE}
        dsem = {}
        for q in self.dma_count:
            for j in range(self.n_dma_sems):
                dsem[(q, j)] = es.enter_context(nc.semaphore("d_%s_%d" % (q, j)))
        cnt = {e: 0 for e in self.COMPUTE}
        for op in self.ops:
            if (not op.dma) and op.sig:
                cnt[op.eng] += 1
                op.sigidx = cnt[op.eng]
        per = {e: [] for e in ("pe", "act", "dve", "pool", "sp")}
        for op in self.ops:
            per[op.eng].append(op)
        final = {}
        for op in self.ops:
            if op.dma:
                final[op.sem] = max(final.get(op.sem, 0), op.semval)
        block = es.enter_context(nc.Block())

        def run(ops, is_last):
            def body(h):
                waited = {}

                def w(sem, key, val):
                    if waited.get(key, 0) >= val:
                        return
                    h.wait_ge(sem, val)
                    waited[key] = val
                for op in ops:
                    for p in op.deps:
                        if p.dma:
                            w(dsem[p.sem], p.sem, p.semval)
                        else:
                            w(esem[p.eng], p.eng, p.sigidx)
                    if op.dma:
                        if op.semval > 16:
                            w(dsem[op.sem], op.sem, op.semval - 16)
                        ins = op.fn(h)
                        ins.then_inc(dsem[op.sem], 16)
                    else:
                        ins = op.fn(h)
                        if op.sig:
                            ins.then_inc(esem[op.eng], 1)
                if is_last:
                    for k, v in final.items():
                        w(dsem[k], k, v)
            return body

        block.tensor(run(per["pe"], False))
        block.scalar(run(per["act"], False))
        block.vector(run(per["dve"], False))
        block.gpsimd(run(per["pool"], False))
        block.sync(run(per["sp"], True))


def _colperm():
    perm = list(range(3072, 3200))
    for j in range(8):
        for base in (0, 1024, 2048, 3200):
            perm += list(range(base + j * 128, base + (j + 1) * 128))
    for j in range(8):
        for base in (4224, 5248):
            perm += list(range(base + j * 128, base + (j + 1) * 128))
    for j in range(8):
        for base in (6272, 7296):
            perm += list(range(base + j * 128, base + (j + 1) * 128))
    for m in range(8):
        for base in (8320, 9344, 10368):
            perm += list(range(base + m * 128, base + (m + 1) * 128))
    assert len(perm) == NCOL
    return np.array(perm)


OFF_LORA = 0
OFF_RW = 128
OFF_LRU = 128 + 4096
OFF_XA = OFF_LRU + 2048
OFF_GATE = OFF_XA + 2048

PV_NAMES = ["g_norm", "mu_r", "mu_k", "mu_v", "w0", "a0", "k_k", "k_a", "r_k", "gn_g", "gn_b",
            "cw0", "cw1", "cw2", "cw3", "conv_b", "b_rg_a", "b_rg_x", "lam", "g_mem", "g_final"]
PVI = {n: 8 * i for i, n in enumerate(PV_NAMES)}
PV_MULORA = 8 * len(PV_NAMES)
NPV = PV_MULORA + 1
DV_NAMES = ["omm_r", "omm_k", "omm_v", "omka", "gn32", "gm32", "gf32", "clam", "clam2"]
DVI = {n: 8 * i for i, n in enumerate(DV_NAMES)}
DV_OMMLORA = 8 * len(DV_NAMES)
NDV = DV_OMMLORA + 1


def _fm(v):
    return np.ascontiguousarray(np.asarray(v, np.float32).reshape(8, 128).T)


def build_nc():
    nc = bass.Bass("TRN2", target_bir_lowering=False)
    din = lambda name, shape: nc.dram_tensor(name, shape, F32, kind="ExternalInput").ap()
    dout = lambda name, shape: nc.dram_tensor(name, shape, F32, kind="ExternalOutput").ap()
    d_xT = din("xT", [128, 8, NTOK])
    d_memT = din("memT", [128, 8, 256])
    d_win = din("w_in_l", [128, 8, NCOL])
    d_wkv = din("w_kv_l", [128, 8, 2048])
    d_wbr = din("w_br_l", [128, 8, 3, 8, 128])
    d_wout = din("w_out_l", [128, 8, 8, 128])
    d_wl = din("wl", [128, 1024])
    d_wrga = din("w_rg_a", [16, 64, 64])
    d_wrgx = din("w_rg_x", [16, 64, 64])
    d_pvec = din("pvec", [128, NPV])
    d_kTs = din("kT_s", [NSEQ, 128, 8, 256])
    d_vs = din("v_s", [NSEQ, 128, 2, 1024])
    d_strw = din("st_rwkv", [128, 8, NSEQ, 64])
    d_stsh = din("st_shift", [128, 25, NSEQ])
    d_stlru = din("st_lru", [128, 8, NSEQ])
    d_stcv = din("st_conv", [128, 8, NSEQ, 3])
    o_yT = dout("yT", [128, 8, NTOK])
    o_rwp = dout("rwkv_p", [128, 8, 1, 64])
    o_shp = dout("shift_p", [128, 25, 1])
    o_lrup = dout("lru_p", [128, 8, 1])
    o_cvp = dout("conv_p", [128, 8, 1, 3])
    o_mk = dout("memk_p", [128, 8, 256])
    o_mv = dout("memv_p", [128, 2, 1024])
    o_rws = dout("rwkv_s", [128, 8, NSEQ, 64])
    o_shs = dout("shift_s", [128, 25, NSEQ])
    o_lrus = dout("lru_s", [128, 8, NSEQ])
    o_cvs = dout("conv_s", [128, 8, NSEQ, 3])

    with ExitStack() as es:
        P = Prog(nc)
        sb = lambda name, shape, dt=F32: es.enter_context(nc.sbuf_tensor(name, shape, dt))
        pst = lambda name, shape, dt=F32: es.enter_context(nc.psum_tensor(name, shape, dt))

        def fsz(ap):
            n_ = 1
            for d_ in ap.shape[1:]:
                n_ *= int(d_)
            return n_

        def dma(q, out, in_, R=(), W=()):
            return P.add(q, lambda h: h.dma_start(out=out, in_=in_), reads=R, writes=W, dma=True, cost=2.0 + fsz(out) * 128 * 4 / 150e3)

        def act(out, in_, func, R, W, bias=None, scale=None, accum=None):
            kw = {}
            if bias is not None:
                kw["bias"] = bias
            if scale is not None:
                kw["scale"] = scale
            if accum is not None:
                kw["accum_out"] = accum
            return P.add("act", lambda h: h.activation(out=out, in_=in_, func=func, **kw), reads=R, writes=W, cost=0.2 + fsz(out) / 1200.0)

        def tt(eng, out, in0, in1, op, R, W):
            return P.add(eng, lambda h: h.tensor_tensor(out=out, in0=in0, in1=in1, op=op), reads=R, writes=W,
                         cost=(0.12 + fsz(out) / 960.0) if eng == "dve" else (0.25 + fsz(out) / 450.0))

        def ts(eng, out, in0, s1, op0, R, W, s2=None, op1=None):
            c_ = (0.12 + fsz(out) / 960.0) if eng == "dve" else (0.3 + fsz(out) / 80.0)
            if op1 is None:
                return P.add(eng, lambda h: h.tensor_scalar(out=out, in0=in0, scalar1=s1, scalar2=None, op0=op0), reads=R, writes=W, cost=c_)
            return P.add(eng, lambda h: h.tensor_scalar(out=out, in0=in0, scalar1=s1, scalar2=s2, op0=op0, op1=op1), reads=R, writes=W, cost=c_)

        def stt(out, in0, scalar, in1, op0, op1, R, W):
            return P.add("dve", lambda h: h.scalar_tensor_tensor(out=out, in0=in0, scalar=scalar, in1=in1, op0=op0, op1=op1), reads=R, writes=W,
                         cost=0.12 + fsz(out) / 960.0)

        def cp(eng, out, in_, R, W):
            if eng == "act":
                return act(out, in_, AF.Copy, R, W)
            return P.add(eng, lambda h: h.tensor_copy(out=out, in_=in_), reads=R, writes=W,
                         cost=(0.12 + fsz(out) / 960.0) if eng == "dve" else (0.25 + fsz(out) / 450.0))

        def mm(out, lhsT, rhs, start, stop, R, W):
            return P.add("pe", lambda h: h.matmul(out, lhsT=lhsT, rhs=rhs, start=start, stop=stop), reads=R, writes=W,
                         cost=(0.035 + fsz(rhs) / 2400.0) * (4.0 if rhs.dtype == F32 else 1.0),
                         rowtile=(lhsT.base_partition(), 128 if lhsT.shape[0] > 64 else (64 if lhsT.shape[0] > 32 else 32)))

        def tr(out, in_, ident, R, W):
            return P.add("pe", lambda h: h.transpose(out=out, in_=in_, identity=ident), reads=R, writes=W, cost=0.05,
                         rowtile=(in_.base_partition(), 128 if in_.shape[0] > 64 else (64 if in_.shape[0] > 32 else 32)))

        def scan(out, d0, d1, R, W):
            return P.add("dve", lambda h: h.tensor_tensor_scan(out=out, data0=d0, data1=d1, initial=0.0, op0=ALU.mult, op1=ALU.add), reads=R, writes=W,
                         cost=0.1 + 2.0 * fsz(out) / 960.0)

        def memset(eng, ap, val, W):
            return P.add(eng, lambda h: h.memset(ap, val), writes=W)

        psM = [pst("psM%d" % i, [128, 512]) for i in range(2)]
        psA = [pst("psA%d" % i, [128, 512]) for i in range(4)]
        psT = [pst("psT%d" % i, [128, 1024], BF16) for i in range(2)]
        rot = {"M": 0, "A": 0, "T": 0, "AP": 0, "A3": 0}

        psTf = [psT[i][:].bitcast(F32) for i in range(2)]
        MNARROW = [(psM[0], "psM0"), (psTf[0], "psT0"), (psM[1], "psM1"), (psTf[1], "psT1")]
        MWIDE = MNARROW + [(psA[i], "psA%d" % i) for i in range(4)]
        mlist = [MNARROW]

        def bank(kind):
            if kind == "M":
                lst_ = mlist[0]
                i = rot[kind] % len(lst_)
                rot[kind] += 1
                return lst_[i]
            lst = {"A": psA, "T": psT}[kind]
            i = rot[kind] % len(lst)
            rot[kind] += 1
            return lst[i], "ps%s%d" % (kind, i)

        def bankpair():
            i = (rot["AP"] % 2) * 2
            rot["AP"] += 1
            return [(psA[i], "psA%d" % i), (psA[i + 1], "psA%d" % (i + 1))]

        pv = sb("pv", [128, NPV])
        dv = sb("dv", [128, NDV])
        ident = sb("ident", [128, 128], BF16)
        onesb = sb("onesb", [128, 128], BF16)
        bd1 = sb("bd1", [128, 128])
        bd64 = sb("bd64", [128, 128])
        bd1b = sb("bd1b", [128, 128], BF16)
        bd64b = sb("bd64b", [128, 128], BF16)
        onesf = sb("onesf", [128, 1, 64])
        masks = {}
        for Cc in (64, 8):
            for nm in ("sl", "su", "ue", "eq"):
                masks[(nm, Cc)] = sb("mk_%s_%d" % (nm, Cc), [128, 1, Cc])
        segmask = {64: sb("segm64", [128, 512]), 8: sb("segm8", [128, 128])}
        wlb = sb("wlb", [128, 1024], BF16)
        wrg = sb("wrg", [128, 8, 2, 128], BF16)
        epsn = sb("epsn", [128, 4])

        dma("sp", pv[:], d_pvec, W=["pv"])
        memset("pool", onesb[:], 1.0, ["onesb"])
        memset("pool", onesf[:], 1.0, ["onesf"])
        P.add("pool", lambda h: h.affine_select(out=ident[:], in_=onesb[:], pattern=[[-1, 128]], compare_op=ALU.is_equal,
                                                fill=0.0, base=0, channel_multiplier=1), reads=["onesb"], writes=["ident"])
        memset("dve", bd1[:], 0.0, ["bd1"])
        memset("dve", bd64[:], 0.0, ["bd64"])
        for hh in range(2):
            memset("dve", bd1[64 * hh:64 * hh + 64, 64 * hh:64 * hh + 64], 1.0, ["bd1"])
            memset("dve", bd64[64 * hh:64 * hh + 64, 64 * hh:64 * hh + 64], 1.0 / 64.0, ["bd64"])
        cp("dve", bd1b[:], bd1[:], ["bd1"], ["bd1b"])
        cp("dve", bd64b[:], bd64[:], ["bd64"], ["bd64b"])
        memset("dve", epsn[:, 0:1], 1e-12, ["epsn"])
        memset("dve", epsn[:, 1:2], 64e-5, ["epsn"])
        memset("dve", epsn[:, 2:3], 1024.0 * 1e-6, ["epsn"])
        memset("dve", epsn[:, 3:4], 1.0, ["epsn"])
        for Cc in (64, 8):
            for hh in range(2):
                sl_ = slice(64 * hh, 64 * hh + 64)
                P.add("pool", lambda h, sl_=sl_, Cc=Cc: h.affine_select(out=masks[("sl", Cc)][sl_], in_=onesf[sl_, :, 0:Cc], pattern=[[0, 1], [-1, Cc]],
                      compare_op=ALU.is_gt, fill=0.0, base=0, channel_multiplier=1), reads=["onesf"], writes=["mk_sl_%d" % Cc])
                P.add("pool", lambda h, sl_=sl_, Cc=Cc: h.affine_select(out=masks[("su", Cc)][sl_], in_=onesf[sl_, :, 0:Cc], pattern=[[0, 1], [1, Cc]],
                      compare_op=ALU.is_gt, fill=0.0, base=0, channel_multiplier=-1), reads=["onesf"], writes=["mk_su_%d" % Cc])
                P.add("pool", lambda h, sl_=sl_, Cc=Cc: h.affine_select(out=masks[("ue", Cc)][sl_], in_=onesf[sl_, :, 0:Cc], pattern=[[0, 1], [1, Cc]],
                      compare_op=ALU.is_ge, fill=0.0, base=0, channel_multiplier=-1), reads=["onesf"], writes=["mk_ue_%d" % Cc])
                P.add("pool", lambda h, sl_=sl_, Cc=Cc: h.affine_select(out=masks[("eq", Cc)][sl_], in_=onesf[sl_, :, 0:Cc], pattern=[[0, 1], [1, Cc]],
                      compare_op=ALU.is_equal, fill=0.0, base=0, channel_multiplier=-1), reads=["onesf"], writes=["mk_eq_%d" % Cc])
        for Cc, n_ in ((64, 512), (8, 128)):
            memset("pool", segmask[Cc][:], 1.0, ["segm%d" % Cc])
            memset("pool", segmask[Cc][:].rearrange("p (a b) -> p a b", b=Cc)[:, :, 0:1], 0.0, ["segm%d" % Cc])
        for nm, src in (("omm_r", "mu_r"), ("omm_k", "mu_k"), ("omm_v", "mu_v"), ("omka", "k_a")):
            ts("dve", dv[:, DVI[nm]:DVI[nm] + 8], pv[:, PVI[src]:PVI[src] + 8], -1.0, ALU.mult, ["pv"], ["dv"], s2=1.0, op1=ALU.add)
        ts("dve", dv[:, DV_OMMLORA:DV_OMMLORA + 1], pv[:, PV_MULORA:PV_MULORA + 1], -1.0, ALU.mult, ["pv"], ["dv"], s2=1.0, op1=ALU.add)
        for nm, src in (("gn32", "g_norm"), ("gm32", "g_mem"), ("gf32", "g_final")):
            ts("dve", dv[:, DVI[nm]:DVI[nm] + 8], pv[:, PVI[src]:PVI[src] + 8], 32.0, ALU.mult, ["pv"], ["dv"])
        act(dv[:, DVI["clam"]:DVI["clam"] + 8], pv[:, PVI["lam"]:PVI["lam"] + 8], AF.Exp, ["pv"], ["dv"], scale=-1.0)
        act(dv[:, DVI["clam"]:DVI["clam"] + 8], dv[:, DVI["clam"]:DVI["clam"] + 8], AF.Ln, ["dv"], ["dv"], bias=epsn[:, 3:4], scale=1.0)
        ts("dve", dv[:, DVI["clam"]:DVI["clam"] + 8], dv[:, DVI["clam"]:DVI["clam"] + 8], -8.0, ALU.mult, ["dv"], ["dv"])
        ts("dve", dv[:, DVI["clam2"]:DVI["clam2"] + 8], dv[:, DVI["clam"]:DVI["clam"] + 8], 2.0, ALU.mult, ["dv"], ["dv"])
        dma("pool", wlb[:], d_wl, W=["wlb"])
        memset("dve", wrg[:], 0.0, ["wrg"])
        for g, src in ((0, d_wrga), (1, d_wrgx)):
            for hh in range(2):
                s_ = src.rearrange("(j h) i o -> h i j o", h=2)[hh]
                dma("pool", wrg[64 * hh:64 * hh + 64, :, g, 64 * hh:64 * hh + 64], s_, W=["wrg"])

        def pcol(name, c):
            i = PVI[name] + c
            return pv[:, i:i + 1]

        def dcol(name, c):
            i = DVI[name] + c
            return dv[:, i:i + 1]

        NMAX = 512
        xT = sb("xTs", [128, 8, NMAX])
        xnT = sb("xnTs", [128, 8, NMAX], BF16)
        oT = [sb("oT%d" % b, [128, 8, NMAX], BF16) for b in range(3)]
        mgT = sb("mgT", [128, 8, NMAX], BF16)
        NF, NB = 18, 41
        Ft = [sb("F%d" % i, [128, 520]) for i in range(NF)]
        Bt = [sb("B%d" % i, [128, 520], BF16) for i in range(NB)]
        NFA = 8
        XK = ["xTf%d" % i for i in range(NFA)]
        freeF = list(range(NF))
        freeFA = list(range(NFA))
        freeB = list(range(NB))

        class T:
            def __init__(self, t, key, idx, pool):
                self.t, self.key, self.idx, self.pool = t, key, idx, pool

            def free(self):
                {"F": freeF, "B": freeB, "FA": freeFA}[self.pool].append(self.idx)

        def fal(wide=False):
            if (not wide) and ALIAS_OK[0] and freeFA:
                i = freeFA.pop(0)
                return T(xT[:, i, :], "xTf%d" % i, i, "FA")
            i = freeF.pop(0)
            return T(Ft[i], "F%d" % i, i, "F")
        ALIAS_OK = [False]

        def bal():
            i = freeB.pop(0)
            return T(Bt[i], "B%d" % i, i, "B")

        class WS:
            def __init__(self, names, width, lookahead):
                self.bufs = [sb(nm, [128, width], BF16) for nm in names]
                self.keys = list(names)
                self.list = []
                self.ptr = 0
                self.loaded = 0
                self.look = lookahead

            def add(self, src, nfree, seg=0):
                self.list.append((src, nfree, seg))

            def load(self, i):
                src, nfree, _ = self.list[i]
                b_ = i % len(self.bufs)
                dst = self.bufs[b_][:, 0:nfree]
                shp = src.shape
                if len(shp) == 3:
                    dst = dst.rearrange("p (a b) -> p a b", a=shp[1])
                elif len(shp) == 4:
                    dst = dst.rearrange("p (a b c) -> p a b c", a=shp[1], b=shp[2])
                dma("pool", dst, src, W=[self.keys[b_]])

            def next(self):
                i = self.ptr
                seg = self.list[i][2]
                while self.loaded < len(self.list) and self.loaded <= i + self.look and (self.loaded <= i or self.list[self.loaded][2] == seg):
                    self.load(self.loaded)
                    self.loaded += 1
                self.ptr += 1
                b_ = i % len(self.bufs)
                return self.bufs[b_], self.keys[b_]

        wR = WS(["wR0", "wR1"], 4096, 1)
        lm_names = ["wLM0", "wLM1", "wLM2"]
        wL = WS(lm_names, 3072, 2)
        wM = WS([], 3072, 1)
        wM.bufs = wL.bufs
        wM.keys = wL.keys

        st = {}
        for kind, ns in (("p", 1), ("s", NSEQ)):
            st[kind] = dict(
                U=sb("cU_" + kind, [128, 25, ns]),
                H=sb("cH_" + kind, [128, 8, ns]),
                CV=sb("cCV_" + kind, [128, 8, ns, 3]),
            )
        st["p"]["RW"] = sb("cRW_p", [128, 8, 1, 64])
        rwj = [sb("rwj%d" % i, [128, NSEQ, 64]) for i in range(2)]
        for nm in ("U", "H", "CV", "RW"):
            memset("pool", st["p"][nm][:], 0.0, ["c%s_p" % nm])
        dma("sp", st["s"]["U"][:], d_stsh, W=["cU_s"])
        dma("sp", st["s"]["H"][:], d_stlru, W=["cH_s"])
        dma("sp", st["s"]["CV"][:], d_stcv, W=["cCV_s"])

        KT = [sb("KT%d" % i, [128, 8, 256], BF16) for i in range(1)]
        VT = [sb("VT%d" % i, [128, 2, 1024], BF16) for i in range(1)]
        NKS = 2
        KTs = [sb("KTs%d" % i, [128, 2, 256], BF16) for i in range(NKS)]
        VTs = [sb("VTs%d" % i, [128, 2, 256], BF16) for i in range(NKS)]
        ksrot = [0]

        sts = [("p", i * 512, 512) for i in range(4)] + [("s", TP, TS)]

        def wsrc(off, n):
            return d_win[:, :, off:off + n], 8 * n
        for i in range(4):
            wR.add(d_wkv[:, :, i * 512:(i + 1) * 512], 8 * 512)
        for si_, _ in enumerate(sts):
            wR.add(*wsrc(OFF_LORA, 128))
            for j in range(8):
                wR.add(*wsrc(OFF_RW + j * 512, 512))
            for j in range(8):
                wL.add(*wsrc(OFF_LRU + j * 256, 256), seg=si_)
            for j in range(8):
                wL.add(*wsrc(OFF_XA + j * 256, 256), seg=si_)
            for m in range(8):
                wM.add(*wsrc(OFF_GATE + m * 384, 384), seg=si_)
                wM.add(d_wbr[:, m], 3 * 8 * 128, seg=si_)
            for oc in range(8):
                wM.add(d_wout[:, oc], 8 * 128, seg=si_)

        def wview(wt, n):
            return wt[:, 0:8 * n].rearrange("p (k c) -> p k c", c=n)

        def proj(wv, col0, n, wkey, ncols=128):
            pb, pk = bank("M")
            for kc in range(8):
                mm(pb[0:ncols, 0:n], wv[:, kc, col0:col0 + ncols], xnT[:, kc, 0:n], kc == 0, kc == 7, [wkey, "xnT"], [pk])
            return pb, pk

        def rsqrt_from(out, in_, eps_col, R, W):
            act(out, in_, AF.Ln, R, W, bias=epsn[:, eps_col:eps_col + 1], scale=1.0)
            act(out, out, AF.Exp, W, W, scale=-0.5)

        def norm_rinv(src_t, src_key, n, nchunk=8):
            sqs = [bal(), bal()]
            pa, pak = bank("A")
            for kc in range(nchunk):
                sq = sqs[kc % 2]
                sqv = sq.t[:, 0:n]
                act(sqv, src_t[:, kc, 0:n], AF.Square, [src_key[kc] if isinstance(src_key, list) else src_key], [sq.key])
                mm(pa[:, 0:n], onesb[:], sqv, kc == 0, kc == nchunk - 1, [sq.key, "onesb"], [pak])
            r = fal()
            rsqrt_from(r.t[:, 0:n], pa[:, 0:n], 2, [pak], [r.key])
            sqs[0].free(); sqs[1].free()
            return r

        for kc in range(8):
            dma("sp", xT[:, kc, 0:256], d_memT[:, kc, :], W=[XK[kc]])
        rm = norm_rinv(xT, XK, 256)
        for kc in range(8):
            stt(xnT[:, kc, 0:256], xT[:, kc, 0:256], dcol("gm32", kc), rm.t[:, 0:256], ALU.mult, ALU.mult, [XK[kc], "dv", rm.key], ["xnT"])
        rm.free()
        for i in range(2):
            wt, wk = wR.next()
            wv = wview(wt, 512)
            for q in range(4):
                oc = i * 4 + q
                pb, pk = bank("M")
                for kc in range(8):
                    mm(pb[:, 0:256], wv[:, kc, q * 128:(q + 1) * 128], xnT[:, kc, 0:256], kc == 0, kc == 7, [wk, "xnT"], [pk])
                kf = fal()
                cp("act", kf.t[:, 0:256], pb[:, 0:256], [pk], [kf.key])
                cp("dve", KT[0][:, oc, :], pb[:, 0:256], [pk], ["KT0"])
                dma("sp", o_mk[:, oc, :], kf.t[:, 0:256], R=[kf.key])
                kf.free()
        for i in range(2):
            wt, wk = wR.next()
            wv = wview(wt, 512)
            for mh in range(2):
                pb, pk = bank("M")
                for kc in range(8):
                    mm(pb[:, 0:512], xnT[:, kc, mh * 128:(mh + 1) * 128], wv[:, kc, :], kc == 0, kc == 7, [wk, "xnT"], [pk])
                vf = fal()
                cp("act", vf.t[:, 0:512], pb[:, 0:512], [pk], [vf.key])
                cp("dve", VT[0][:, mh, i * 512:(i + 1) * 512], pb[:, 0:512], [pk], ["VT0"])
                dma("sp", o_mv[:, mh, i * 512:(i + 1) * 512], vf.t[:, 0:512], R=[vf.key])
                vf.free()

        for (kind, tok0, n) in sts:
            sample = kind == "s"
            NSEG = NSEQ if sample else 1
            SL = SLEN if sample else 512
            C = 8 if sample else 64
            NCH = n // C
            S = st[kind]
            kU, kH, kCV = "cU_" + kind, "cH_" + kind, "cCV_" + kind

            def v3(ap, a=NSEG):
                return ap.rearrange("p (a b) -> p a b", a=a)

            for kc in range(8):
                dma("sp", xT[:, kc, 0:n], d_xT[:, kc, tok0:tok0 + n], W=[XK[kc]])
            rr = norm_rinv(xT, XK, n)
            for kc in range(8):
                stt(xnT[:, kc, 0:n], xT[:, kc, 0:n], dcol("gn32", kc), rr.t[:, 0:n], ALU.mult, ALU.mult, [XK[kc], "dv", rr.key], ["xnT"])
            rr.free()
            ALIAS_OK[0] = True

            def shifted(pb, pk, ucol, mu_ap, omm_ap):
                xs = fal(True)
                X3 = xs.t[:, 0:NSEG * (SL + 1)].rearrange("p (a b) -> p a b", b=SL + 1)
                act(X3[:, :, 1:SL + 1], v3(pb[:, 0:n]), AF.Copy, [pk], [xs.key])
                act(X3[:, :, 0:1], S["U"][:, ucol, :].rearrange("p (a b) -> p a b", b=1), AF.Copy, [kU], [xs.key])
                res = fal()
                act(res.t[:, 0:n], pb[:, 0:n], AF.Identity, [pk, "dv"], [res.key], scale=omm_ap)
                stt(v3(res.t[:, 0:n]), X3[:, :, 0:SL], mu_ap, v3(res.t[:, 0:n]), ALU.mult, ALU.add, [xs.key, "pv", res.key], [res.key])
                cp("pool", S["U"][:, ucol, :].rearrange("p (a b) -> p a b", b=1), X3[:, :, SL:SL + 1], [xs.key], [kU])
                xs.free()
                return res

            wr_base = wR.ptr
            wt, wk = wR.next()
            wv = wview(wt, 128)
            pb, pk = proj(wv, 0, n, wk)
            lo = shifted(pb, pk, 24, pv[:, PV_MULORA:PV_MULORA + 1], dv[:, DV_OMMLORA:DV_OMMLORA + 1])
            lorab = bal()
            act(lorab.t[0:64, 0:n], lo.t[0:64, 0:n], AF.Tanh, [lo.key], [lorab.key])
            cp("dve", lorab.t[64:128, 0:n], lo.t[64:128, 0:n], [lo.key], [lorab.key])
            lo.free()

            def rwkv_hp(j):
                while wR.ptr < wr_base + 1 + j:
                    yield
                wt, wk = wR.next()
                wv = wview(wt, 512)
                cs = slice(j * 128, (j + 1) * 128)
                pb, pk = proj(wv, 0, n, wk)
                r_s = shifted(pb, pk, j, pcol("mu_r", j), dcol("omm_r", j))
                pb, pk = proj(wv, 128, n, wk)
                k_s = shifted(pb, pk, 8 + j, pcol("mu_k", j), dcol("omm_k", j))
                pb, pk = proj(wv, 256, n, wk)
                v_s = shifted(pb, pk, 16 + j, pcol("mu_v", j), dcol("omm_v", j))
                pb, pk = proj(wv, 384, n, wk)
                sz = fal()
                act(sz.t[:, 0:n], pb[:, 0:n], AF.Silu, [pk], [sz.key])
                yield
                pb, pk = bank("M")
                mm(pb[:, 0:n], wlb[0:64, cs], lorab.t[0:64, 0:n], True, True, ["wlb", lorab.key], [pk])
                sig = fal()
                act(sig.t[:, 0:n], pb[:, 0:n], AF.Sigmoid, [pk, "pv"], [sig.key], bias=pcol("w0", j), scale=1.0)
                pb, pk = bank("M")
                mm(pb[:, 0:n], wlb[64:128, cs], lorab.t[64:128, 0:n], True, True, ["wlb", lorab.key], [pk])
                alpha = fal()
                act(alpha.t[:, 0:n], pb[:, 0:n], AF.Sigmoid, [pk, "pv"], [alpha.key], bias=pcol("a0", j), scale=1.0)
                yield
                cum = fal()
                scan(cum.t[:, 0:n], segmask[C][:, 0:n], sig.t[:, 0:n], [sig.key, "segm%d" % C], [cum.key])
                cumex = fal()
                tt("pool", cumex.t[:, 0:n], cum.t[:, 0:n], sig.t[:, 0:n], ALU.subtract, [cum.key, sig.key], [cumex.key])
                sig.free()
                dlt = fal()
                c3 = cum.t[:, 0:n].rearrange("p (a b) -> p a b", b=C)
                tt("pool", dlt.t[:, 0:n].rearrange("p (a b) -> p a b", b=C), c3[:, :, C - 1:C].broadcast_to([128, NCH, C]), c3, ALU.subtract, [cum.key], [dlt.key])
                Epos = fal(); Eneg = fal()
                act(Epos.t[:, 0:n], cum.t[:, 0:n], AF.Exp, [cum.key], [Epos.key], scale=-C0)
                act(Eneg.t[:, 0:n], cum.t[:, 0:n], AF.Exp, [cum.key], [Eneg.key], scale=C0)
                act(cumex.t[:, 0:n], cumex.t[:, 0:n], AF.Exp, [cumex.key], [cumex.key], scale=-C0)
                act(dlt.t[:, 0:n], dlt.t[:, 0:n], AF.Exp, [dlt.key], [dlt.key], scale=-C0)
                cum.free()
                Eex, Ehat = cumex, dlt
                yield
                kksq = bal()
                act(kksq.t[:, 0:n], k_s.t[:, 0:n], AF.Square, [k_s.key, "pv"], [kksq.key], scale=pcol("k_k", j))
                pb, pk = bank("M")
                mm(pb[:, 0:n], bd1b[:], kksq.t[:, 0:n], True, True, ["bd1b", kksq.key], [pk])
                kksq.free()
                rn = fal()
                rsqrt_from(rn.t[:, 0:n], pb[:, 0:n], 0, [pk], [rn.key])
                kk = fal()
                stt(kk.t[:, 0:n], k_s.t[:, 0:n], pcol("k_k", j), rn.t[:, 0:n], ALU.mult, ALU.mult, [k_s.key, "pv", rn.key], [kk.key])
                rn.free()
                yield
                kmod = fal()
                act(kmod.t[:, 0:n], alpha.t[:, 0:n], AF.Identity, [alpha.key, "pv", "dv"], [kmod.key], bias=dcol("omka", j), scale=pcol("k_a", j))
                tt("pool", kmod.t[:, 0:n], kmod.t[:, 0:n], k_s.t[:, 0:n], ALU.mult, [kmod.key, k_s.key], [kmod.key])
                k_s.free()
                bvec = fal()
                tt("dve", bvec.t[:, 0:n], kk.t[:, 0:n], alpha.t[:, 0:n], ALU.mult, [kk.key, alpha.key], [bvec.key])
                alpha.free()
                yield
                aT = bal(); btT = bal(); bhT = bal(); ktT = bal(); khT = bal(); rtT = bal(); vb = bal()
                stt(aT.t[:, 0:n], kk.t[:, 0:n], -1.0, Eex.t[:, 0:n], ALU.mult, ALU.mult, [kk.key, Eex.key], [aT.key])
                tt("dve", btT.t[:, 0:n], bvec.t[:, 0:n], Eneg.t[:, 0:n], ALU.mult, [bvec.key, Eneg.key], [btT.key])
                tt("dve", bhT.t[:, 0:n], bvec.t[:, 0:n], Ehat.t[:, 0:n], ALU.mult, [bvec.key, Ehat.key], [bhT.key])
                tt("pool", ktT.t[:, 0:n], kmod.t[:, 0:n], Eneg.t[:, 0:n], ALU.mult, [kmod.key, Eneg.key], [ktT.key])
                tt("pool", khT.t[:, 0:n], kmod.t[:, 0:n], Ehat.t[:, 0:n], ALU.mult, [kmod.key, Ehat.key], [khT.key])
                tt("dve", rtT.t[:, 0:n], r_s.t[:, 0:n], Epos.t[:, 0:n], ALU.mult, [r_s.key, Epos.key], [rtT.key])
                cp("act", vb.t[:, 0:n], v_s.t[:, 0:n], [v_s.key], [vb.key])
                kk.free(); bvec.free(); Eex.free(); Ehat.free(); Eneg.free()
                yield
                rk = bal()
                stt(rk.t[:, 0:n], r_s.t[:, 0:n], pcol("r_k", j), kmod.t[:, 0:n], ALU.mult, ALU.mult, [r_s.key, "pv", kmod.key], [rk.key])
                r_s.free(); kmod.free()
                pb, pk = bank("M")
                mm(pb[:, 0:n], bd1b[:], rk.t[:, 0:n], True, True, ["bd1b", rk.key], [pk])
                bonus = fal()
                tt("dve", bonus.t[:, 0:n], pb[:, 0:n], v_s.t[:, 0:n], ALU.mult, [pk, v_s.key], [bonus.key])
                rk.free(); v_s.free()
                yield

                if sample:
                    RWt, kRW = rwj[j % 2], "rwj%d" % (j % 2)
                    dma("sp", RWt[:], d_strw[:, j], W=[kRW])
                else:
                    RWt, kRW = st["p"]["RW"][:, j], "cRW_p"
                osb = fal()
                nbatch = NCH // 8
                msl, msu, mue, meq = ((masks[(nm, C)], "mk_%s_%d" % (nm, C)) for nm in ("sl", "su", "ue", "eq"))
                evc = [0]

                def evac_eng():
                    evc[0] += 1
                    return "act" if evc[0] % 2 else "dve"
                for b in range(nbatch):
                    def cols(cc):
                        c = b * 8 + cc
                        return slice(c * C, (c + 1) * C)
                    toks = {}
                    for nm, src in (("A", aT), ("Bh", bhT), ("Kh", khT), ("V", vb)):
                        tk = bal()
                        for hh in range(2):
                            pt, ptk = psT[hh], "psT%d" % hh
                            for cc in range(8):
                                tr(pt[64 * hh:64 * hh + C, cc * 64:(cc + 1) * 64], src.t[64 * hh:64 * hh + 64, cols(cc)],
                                   ident[64 * hh:64 * hh + 64, 64 * hh:64 * hh + 64], [src.key, "ident"], [ptk])
                            cp(evac_eng(), tk.t[64 * hh:64 * hh + C, 0:512], pt[64 * hh:64 * hh + C, 0:512], [ptk], [tk.key])
                        toks[nm] = tk
                        yield
                    if b == nbatch - 1:
                        bhT.free(); khT.free(); vb.free()

                    def amat(L, R, mask, pool_="A"):
                        pp = bankpair()
                        o = bal()
                        for hh in range(2):
                            pa, pak = pp[hh]
                            hs_ = slice(64 * hh, 64 * hh + 64)
                            for cc in range(8):
                                mm(pa[64 * hh:64 * hh + C, cc * C:(cc + 1) * C], L.t[hs_, cols(cc)], R.t[hs_, cols(cc)],
                                   True, True, [L.key, R.key], [pak])
                            tt("dve", o.t[hs_, 0:8 * C].rearrange("p (a b) -> p a b", b=C), pa[hs_, 0:8 * C].rearrange("p (a b) -> p a b", b=C),
                               mask[0][hs_].broadcast_to([64, 8, C]), ALU.mult, [pak, mask[1]], [o.key])
                        return o

                    def mmat(L, R, lw, rw, addto=None):
                        pp = bankpair()
                        o = bal()
                        for hh in range(2):
                            pa, pak = pp[hh]
                            hs_ = slice(64 * hh, 64 * hh + 64)
                            for cc in range(8):
                                mm(pa[64 * hh:64 * hh + lw, cc * rw:(cc + 1) * rw],
                                   L.t[64 * hh:64 * hh + C, cc * lw:(cc + 1) * lw], R.t[64 * hh:64 * hh + C, cc * rw:(cc + 1) * rw],
                                   True, True, [L.key, R.key], [pak])
                            if addto is None:
                                cp(evac_eng(), o.t[hs_, 0:8 * rw], pa[hs_, 0:8 * rw], [pak], [o.key])
                            else:
                                tt("dve", o.t[hs_, 0:8 * rw], pa[hs_, 0:8 * rw], addto.t[hs_, 0:8 * rw], ALU.add, [pak, addto.key], [o.key])
                        return o
                    def multi(specs):
                        assert len(specs) >= 2
                        banks = [bank("A") for _ in specs]
                        outs = [bal() for _ in specs]
                        for hh in range(2):
                            hs_ = slice(64 * hh, 64 * hh + 64)
                            for sp, (pa, pak) in zip(specs, banks):
                                if sp[0] == "a":
                                    _, L, R, mask = sp
                                    for cc in range(8):
                                        mm(pa[64 * hh:64 * hh + C, cc * C:(cc + 1) * C], L.t[hs_, cols(cc)], R.t[hs_, cols(cc)],
                                           True, True, [L.key, R.key], [pak])
                                else:
                                    _, L, R, lw, rw, addto = sp
                                    for cc in range(8):
                                        mm(pa[64 * hh:64 * hh + lw, cc * rw:(cc + 1) * rw],
                                           L.t[64 * hh:64 * hh + C, cc * lw:(cc + 1) * lw], R.t[64 * hh:64 * hh + C, cc * rw:(cc + 1) * rw],
                                           True, True, [L.key, R.key], [pak])
                        for sp, (pa, pak), o in zip(specs, banks, outs):
                            nrow = C if sp[0] == "a" else sp[3]
                            rsl = [slice(0, 128)] if nrow == 64 else [slice(0, nrow), slice(64, 64 + nrow)]
                            for r_ in rsl:
                                np_ = r_.stop - r_.start
                                if sp[0] == "a":
                                    mask = sp[3]
                                    tt("dve", o.t[r_, 0:8 * C].rearrange("p (a b) -> p a b", b=C), pa[r_, 0:8 * C].rearrange("p (a b) -> p a b", b=C),
                                       mask[0][r_].broadcast_to([np_, 8, C]), ALU.mult, [pak, mask[1]], [o.key])
                                else:
                                    rw, addto = sp[4], sp[5]
                                    if addto is None:
                                        cp(evac_eng(), o.t[r_, 0:8 * rw], pa[r_, 0:8 * rw], [pak], [o.key])
                                    else:
                                        tt("dve", o.t[r_, 0:8 * rw], pa[r_, 0:8 * rw], addto.t[r_, 0:8 * rw], ALU.add, [pak, addto.key], [o.key])
                        return outs

                    Pc, PTc, AakT = multi([("a", aT, btT, msl), ("a", btT, aT, msu), ("a", ktT, aT, msu)])
                    yield
                    ArbT, ArkT = multi([("a", btT, rtT, mue), ("a", ktT, rtT, mue)])
                    yield
                    if b == nbatch - 1:
                        aT.free(); btT.free(); ktT.free()
                    Q = bal()
                    for r_ in ([slice(0, 128)] if C == 64 else [slice(0, C), slice(64, 64 + C)]):
                        np_ = r_.stop - r_.start
                        tt("pool", Q.t[r_, 0:8 * C].rearrange("p (a b) -> p a b", b=C), PTc.t[r_, 0:8 * C].rearrange("p (a b) -> p a b", b=C),
                           meq[0][r_].broadcast_to([np_, 8, C]), ALU.add, [PTc.key, meq[1]], [Q.key])
                    lvl = 2
                    pendP = None
                    while lvl < C:
                        last = (lvl * 2 >= C)
                        specs = [("m", PTc, Pc, C, C, None)]
                        if not last:
                            specs.append(("m", Pc, PTc, C, C, None))
                        if pendP is not None:
                            specs.append(("m", pendP, Q, C, C, Q))
                        if len(specs) == 1:
                            specs.append(("m", Pc, PTc, C, C, None))
                        res = multi(specs)
                        yield
                        Pn = res[0]
                        PTn = res[1] if not last else None
                        extra_unused = None
                        if last and pendP is None:
                            extra_unused = res[1]
                        if pendP is not None:
                            Qn = res[-1]
                            Q.free(); pendP.free()
                            Q = Qn
                        elif extra_unused is not None:
                            extra_unused.free()
                        PTc.free()
                        pendP_old = Pc
                        if pendP is None:
                            Pc.free()
                        pendP = Pn
                        Pc, PTc = Pn, PTn
                        lvl *= 2
                    res = multi([("m", pendP, Q, C, C, Q), ("m", AakT, toks["V"], C, 64, None)])
                    yield
                    Q.free(); pendP.free(); AakT.free()
                    Q, Xv = res[0], res[1]
                    if PTc is not None:
                        PTc.free()
                    WT = mmat(toks["A"], Q, 64, C)
                    toks["A"].free()
                    yield
                    if sample:
                        Hall = RWt[:, b * 8:(b + 1) * 8, :]
                        Hb = bal(); Ut = bal()
                        cp("act", Hb.t[:, 0:512].rearrange("p (a b) -> p a b", b=64), Hall, [kRW], [Hb.key])
                        for hh in range(2):
                            pu, puk = psA[hh], "psA%d" % hh
                            rs = slice(64 * hh, 64 * hh + C)
                            fs = slice(64 * hh, 64 * hh + 64)
                            for cc in range(8):
                                mm(pu[rs, cc * 64:(cc + 1) * 64], Q.t[rs, cc * C:(cc + 1) * C], Xv.t[rs, cc * 64:(cc + 1) * 64], True, False, [Q.key, Xv.key], [puk])
                                mm(pu[rs, cc * 64:(cc + 1) * 64], WT.t[fs, cc * C:(cc + 1) * C], Hb.t[fs, cc * 64:(cc + 1) * 64], False, True, [WT.key, Hb.key], [puk])
                        for hh in range(2):
                            fs = slice(64 * hh, 64 * hh + 64)
                            rs = slice(64 * hh, 64 * hh + C)
                            cp("dve" if hh == 0 else "act", Ut.t[rs, 0:512], psA[hh][rs, 0:512], ["psA%d" % hh], [Ut.key])
                        yield
                        for hh in range(2):
                            ph, phk = psA[2 + hh], "psA%d" % (2 + hh)
                            rs = slice(64 * hh, 64 * hh + C)
                            fs = slice(64 * hh, 64 * hh + 64)
                            for cc in range(8):
                                mm(ph[fs, cc * 64:(cc + 1) * 64], toks["Kh"].t[rs, cc * 64:(cc + 1) * 64], toks["V"].t[rs, cc * 64:(cc + 1) * 64], True, False,
                                   [toks["Kh"].key, toks["V"].key], [phk])
                                mm(ph[fs, cc * 64:(cc + 1) * 64], toks["Bh"].t[rs, cc * 64:(cc + 1) * 64], Ut.t[rs, cc * 64:(cc + 1) * 64], False, True,
                                   [toks["Bh"].key, Ut.key], [phk])
                        for hh in range(2):
                            pO, pOk = psA[hh], "psA%d" % hh
                            rs = slice(64 * hh, 64 * hh + C)
                            fs = slice(64 * hh, 64 * hh + 64)
                            for cc in range(8):
                                c = b * 8 + cc
                                mm(pO[fs, cc * C:(cc + 1) * C], Hb.t[fs, cc * 64:(cc + 1) * 64], rtT.t[fs, c * C:(c + 1) * C], True, False, [Hb.key, rtT.key], [pOk])
                                mm(pO[fs, cc * C:(cc + 1) * C], toks["V"].t[rs, cc * 64:(cc + 1) * 64], ArkT.t[rs, cc * C:(cc + 1) * C], False, False,
                                   [toks["V"].key, ArkT.key], [pOk])
                                mm(pO[fs, cc * C:(cc + 1) * C], Ut.t[rs, cc * 64:(cc + 1) * 64], ArbT.t[rs, cc * C:(cc + 1) * C], False, True, [Ut.key, ArbT.key], [pOk])
                        gCall = Epos.t[:, 0:n].rearrange("p (a b) -> p a b", b=C)[:, b * 8:(b + 1) * 8, C - 1:C].broadcast_to([128, 8, 64])
                        tt("pool", Hall, Hall, gCall, ALU.mult, [kRW, Epos.key], [kRW])
                        for hh in range(2):
                            fs = slice(64 * hh, 64 * hh + 64)
                            tt("dve", Hall[fs], Hall[fs], psA[2 + hh][fs, 0:512].rearrange("p (a b) -> p a b", b=64), ALU.add, [kRW, "psA%d" % (2 + hh)], [kRW])
                            cp("act", osb.t[fs, b * 8 * C:(b + 1) * 8 * C], psA[hh][fs, 0:8 * C], ["psA%d" % hh], [osb.key])
                        Hb.free(); Ut.free()
                        yield
                    Hb_next = None
                    for cc in (range(8) if not sample else ()):
                        c = b * 8 + cc
                        seg = c if sample else 0
                        Hf = RWt[:, seg, :]
                        Ut = bal()
                        if sample or Hb_next is None:
                            Hb = bal()
                            cp("act", Hb.t[:, 0:64], Hf, [kRW], [Hb.key])
                        else:
                            Hb = Hb_next
                        for hh in range(2):
                            pu, puk = psA[hh], "psA%d" % hh
                            rs = slice(64 * hh, 64 * hh + C)
                            fs = slice(64 * hh, 64 * hh + 64)
                            mm(pu[rs, 0:64], Q.t[rs, cc * C:(cc + 1) * C], Xv.t[rs, cc * 64:(cc + 1) * 64], True, False, [Q.key, Xv.key], [puk])
                            mm(pu[rs, 0:64], WT.t[fs, cc * C:(cc + 1) * C], Hb.t[fs, 0:64], False, True, [WT.key, Hb.key], [puk])
                        for hh in range(2):
                            fs = slice(64 * hh, 64 * hh + 64)
                            cp("dve" if hh == 0 else "act", Ut.t[fs, 0:64], psA[hh][fs, 0:64], ["psA%d" % hh], [Ut.key])
                        for hh in range(2):
                            ph, phk = psA[2 + hh], "psA%d" % (2 + hh)
                            rs = slice(64 * hh, 64 * hh + C)
                            fs = slice(64 * hh, 64 * hh + 64)
                            mm(ph[fs, 0:64], toks["Kh"].t[rs, cc * 64:(cc + 1) * 64], toks["V"].t[rs, cc * 64:(cc + 1) * 64], True, False,
                               [toks["Kh"].key, toks["V"].key], [phk])
                            mm(ph[fs, 0:64], toks["Bh"].t[rs, cc * 64:(cc + 1) * 64], Ut.t[rs, 0:64], False, True, [toks["Bh"].key, Ut.key], [phk])
                        for hh in range(2):
                            pO, pOk = psA[2 + hh], "psA%d" % (2 + hh)
                            rs = slice(64 * hh, 64 * hh + C)
                            fs = slice(64 * hh, 64 * hh + 64)
                            mm(pO[fs, 64:64 + C], Hb.t[fs, 0:64], rtT.t[fs, c * C:(c + 1) * C], True, False, [Hb.key, rtT.key], [pOk])
                            mm(pO[fs, 64:64 + C], toks["V"].t[rs, cc * 64:(cc + 1) * 64], ArkT.t[rs, cc * C:(cc + 1) * C], False, False,
                               [toks["V"].key, ArkT.key], [pOk])
                            mm(pO[fs, 64:64 + C], Ut.t[rs, 0:64], ArbT.t[rs, cc * C:(cc + 1) * C], False, True, [Ut.key, ArbT.key], [pOk])
                        gC = Epos.t[:, c * C + C - 1:c * C + C]
                        Hb_next = None
                        if (not sample) and cc < 7:
                            Hb_next = bal()
                            for hh in range(2):
                                fs = slice(64 * hh, 64 * hh + 64)
                                stt(Hb_next.t[fs, 0:64], Hf[fs], gC[fs], psA[2 + hh][fs, 0:64], ALU.mult, ALU.add, [kRW, Epos.key, "psA%d" % (2 + hh)], [Hb_next.key])
                        for hh in range(2):
                            fs = slice(64 * hh, 64 * hh + 64)
                            stt(Hf[fs], Hf[fs], gC[fs], psA[2 + hh][fs, 0:64], ALU.mult, ALU.add, [kRW, Epos.key, "psA%d" % (2 + hh)], [kRW])
                        for hh in range(2):
                            fs = slice(64 * hh, 64 * hh + 64)
                            cp("act", osb.t[fs, c * C:(c + 1) * C], psA[2 + hh][fs, 64:64 + C], ["psA%d" % (2 + hh)], [osb.key])
                        Hb.free(); Ut.free()
                        yield
                    for t_ in (Q, Xv, WT, ArbT, ArkT, toks["Bh"], toks["Kh"], toks["V"]):
                        t_.free()
                for t_ in (rtT, Epos):
                    t_.free()
                if sample:
                    dma("sp", o_rws[:, j], RWt[:], R=[kRW])
                pb, pk = bank("M")
                mm(pb[:, 0:n], bd64[:], osb.t[:, 0:n], True, True, ["bd64", osb.key], [pk])
                dd = fal()
                tt("dve", dd.t[:, 0:n], osb.t[:, 0:n], pb[:, 0:n], ALU.subtract, [osb.key, pk], [dd.key])
                yield
                dsq = bal()
                act(dsq.t[:, 0:n], dd.t[:, 0:n], AF.Square, [dd.key], [dsq.key])
                pb, pk = bank("M")
                mm(pb[:, 0:n], bd64b[:], dsq.t[:, 0:n], True, True, ["bd64b", dsq.key], [pk])
                dsq.free()
                rstd = fal()
                rsqrt_from(rstd.t[:, 0:n], pb[:, 0:n], 1, [pk], [rstd.key])
                yield
                stt(dd.t[:, 0:n], dd.t[:, 0:n], pcol("gn_g", j), rstd.t[:, 0:n], ALU.mult, ALU.mult, [dd.key, "pv", rstd.key], [dd.key])
                stt(dd.t[:, 0:n], dd.t[:, 0:n], pcol("gn_b", j), bonus.t[:, 0:n], ALU.add, ALU.add, [dd.key, "pv", bonus.key], [dd.key])
                tt("dve", oT[0][:, j, 0:n], dd.t[:, 0:n], sz.t[:, 0:n], ALU.mult, [dd.key, sz.key], ["oT0"])
                for t_ in (osb, dd, rstd, bonus, sz):
                    t_.free()
                yield

            def R_stream(par, delay):
                for _ in range(delay):
                    yield
                for j_ in range(par, 8, 2):
                    yield from rwkv_hp(j_)

            def lru_chunk(j):
                wt, wk = wL.next()
                wv = wview(wt, 256)
                pb, pk = proj(wv, 0, n, wk)
                xb = fal(True)
                XB3 = xb.t[:, 0:NSEG * (SL + 3)].rearrange("p (a b) -> p a b", b=SL + 3)
                act(XB3[:, :, 3:SL + 3], v3(pb[:, 0:n]), AF.Copy, [pk], [xb.key])
                cp("act", XB3[:, :, 0:3], S["CV"][:, j, :, :], [kCV], [xb.key])
                pb, pk = proj(wv, 128, n, wk)
                szb = fal()
                act(szb.t[:, 0:n], pb[:, 0:n], AF.Silu, [pk], [szb.key])
                yield
                xc = fal()
                ts("dve", v3(xc.t[:, 0:n]), XB3[:, :, 0:SL], pcol("cw0", j), ALU.mult, [xb.key, "pv"], [xc.key], s2=pcol("conv_b", j), op1=ALU.add)
                for q in (1, 2, 3):
                    stt(v3(xc.t[:, 0:n]), XB3[:, :, q:q + SL], pcol("cw%d" % q, j), v3(xc.t[:, 0:n]), ALU.mult, ALU.add, [xb.key, "pv", xc.key], [xc.key])
                cp("pool", S["CV"][:, j, :, :], XB3[:, :, SL:SL + 3], [xb.key], [kCV])
                xb.free()
                xcb = bal()
                cp("act", xcb.t[:, 0:n], xc.t[:, 0:n], [xc.key], [xcb.key])
                yield
                pb, pk = bank("M")
                mm(pb[:, 0:n], wrg[:, j, 0, :], xcb.t[:, 0:n], True, True, ["wrg", xcb.key], [pk])
                gr = fal()
                act(gr.t[:, 0:n], pb[:, 0:n], AF.Sigmoid, [pk, "pv"], [gr.key], bias=pcol("b_rg_a", j), scale=1.0)
                pb, pk = bank("M")
                mm(pb[:, 0:n], wrg[:, j, 1, :], xcb.t[:, 0:n], True, True, ["wrg", xcb.key], [pk])
                gi = fal()
                act(gi.t[:, 0:n], pb[:, 0:n], AF.Sigmoid, [pk, "pv"], [gi.key], bias=pcol("b_rg_x", j), scale=1.0)
                xcb.free()
                yield
                aa = fal(); a2 = fal()
                act(aa.t[:, 0:n], gr.t[:, 0:n], AF.Exp, [gr.key, "dv"], [aa.key], scale=dcol("clam", j))
                act(a2.t[:, 0:n], gr.t[:, 0:n], AF.Exp, [gr.key, "dv"], [a2.key], scale=dcol("clam2", j))
                ts("dve", a2.t[:, 0:n], a2.t[:, 0:n], -1.0, ALU.mult, [a2.key], [a2.key], s2=1.0, op1=ALU.add)
                act(a2.t[:, 0:n], a2.t[:, 0:n], AF.Sqrt, [a2.key], [a2.key])
                tt("dve", gi.t[:, 0:n], gi.t[:, 0:n], a2.t[:, 0:n], ALU.mult, [gi.key, a2.key], [gi.key])
                tt("dve", gi.t[:, 0:n], gi.t[:, 0:n], xc.t[:, 0:n], ALU.mult, [gi.key, xc.key], [gi.key])
                xc.free(); gr.free(); a2.free()
                yield
                a3 = v3(aa.t[:, 0:n]); b3 = v3(gi.t[:, 0:n])
                tmp = fal()
                h0 = S["H"][:, j, :].rearrange("p (a b) -> p a b", b=1)
                tt("pool", tmp.t[:, 0:NSEG].rearrange("p (a b) -> p a b", b=1), a3[:, :, 0:1], h0, ALU.mult, [aa.key, kH], [tmp.key])
                tt("pool", b3[:, :, 0:1], b3[:, :, 0:1], tmp.t[:, 0:NSEG].rearrange("p (a b) -> p a b", b=1), ALU.add, [gi.key, tmp.key], [gi.key])
                memset("pool", a3[:, :, 0:1], 0.0, [aa.key])
                tmp.free()
                hs = fal()
                scan(hs.t[:, 0:n], aa.t[:, 0:n], gi.t[:, 0:n], [aa.key, gi.key], [hs.key])
                cp("pool", h0, v3(hs.t[:, 0:n])[:, :, SL - 1:SL], [hs.key], [kH])
                tt("dve", oT[1][:, j, 0:n], hs.t[:, 0:n], szb.t[:, 0:n], ALU.mult, [hs.key, szb.key], ["oT1"])
                for t_ in (aa, gi, hs, szb):
                    t_.free()
                yield

            def attn_head(hd):
                qT = bal(); szc = [fal(), fal()]
                for dc in range(2):
                    wt, wk = wL.next()
                    wv = wview(wt, 256)
                    pb, pk = proj(wv, 0, n, wk)
                    if dc == 0:
                        qd0 = qT
                        act(qd0.t[:, 0:n], pb[:, 0:n], AF.Identity, [pk], [qd0.key], scale=0.0625)
                    else:
                        qd1 = bal()
                        act(qd1.t[:, 0:n], pb[:, 0:n], AF.Identity, [pk], [qd1.key], scale=0.0625)
                    pb, pk = proj(wv, 128, n, wk)
                    act(szc[dc].t[:, 0:n], pb[:, 0:n], AF.Silu, [pk], [szc[dc].key])
                yield
                qd = [qd0, qd1]
                pTall = [bal(), bal()]
                if not sample:
                    groups = [(i * 128, 128, 0) for i in range(n // 128)]
                else:
                    groups = [(s_ * SLEN, SLEN, s_) for s_ in range(NSEQ)]
                ocr = fal() if sample else None
                for (t0, tq, sidx) in groups:
                    if sample:
                        kb = ksrot[0] % NKS
                        ksrot[0] += 1
                        dma("pool", KTs[kb][:], d_kTs[sidx, :, hd * 2:hd * 2 + 2, :], W=["KTs%d" % kb])
                        dma("pool", VTs[kb][:], d_vs[sidx, :, :, hd * 256:(hd + 1) * 256], W=["VTs%d" % kb])
                        ktile, kkey, vtile, vkey = KTs[kb], "KTs%d" % kb, VTs[kb], "VTs%d" % kb
                        kofs, vofs = 0, 0
                    else:
                        ktile, kkey, vtile, vkey = KT[0], "KT0", VT[0], "VT0"
                        kofs, vofs = hd * 2, hd * 256
                    ai = rot["A3"] % 3
                    rot["A3"] += 1
                    pa, pak = psA[ai], "psA%d" % ai
                    for dc in range(2):
                        mm(pa[0:tq, 0:256], qd[dc].t[:, t0:t0 + tq], ktile[:, kofs + dc, :], dc == 0, dc == 1, [qd[dc].key, kkey], [pak])
                    mx = fal()
                    P.add("dve", lambda h, mx=mx, pa=pa, tq=tq: h.tensor_reduce(out=mx.t[0:tq, 0:1], in_=pa[0:tq, 0:256], axis=AX.X, op=ALU.max), reads=[pak], writes=[mx.key])
                    ts("dve", mx.t[0:tq, 1:2], mx.t[0:tq, 0:1], -1.0, ALU.mult, [mx.key], [mx.key])
                    pe_ = bal()
                    act(pe_.t[0:tq, 0:256], pa[0:tq, 0:256], AF.Exp, [pak, mx.key], [pe_.key, mx.key], bias=mx.t[0:tq, 1:2], scale=1.0, accum=mx.t[0:tq, 2:3])
                    P.add("dve", lambda h, mx=mx, tq=tq: h.reciprocal(out=mx.t[0:tq, 3:4], in_=mx.t[0:tq, 2:3]), reads=[mx.key], writes=[mx.key])
                    ts("dve", pe_.t[0:tq, 0:256], pe_.t[0:tq, 0:256], mx.t[0:tq, 3:4], ALU.mult, [pe_.key, mx.key], [pe_.key])
                    pt, ptk = bank("T")
                    for mh in range(2):
                        tr(pt[:, mh * 128:mh * 128 + tq], pe_.t[0:tq, mh * 128:(mh + 1) * 128], ident[0:tq, 0:tq], [pe_.key, "ident"], [ptk])
                    for mh in range(2):
                        cp("act" if mh == 0 else "dve", pTall[mh].t[:, t0:t0 + tq], pt[:, mh * 128:mh * 128 + tq], [ptk], [pTall[mh].key])
                    mx.free(); pe_.free()
                    if sample:
                        pb, pk = bank("M")
                        for dc in range(2):
                            for mh in range(2):
                                mm(pb[:, dc * tq:(dc + 1) * tq], vtile[:, mh, vofs + dc * 128:vofs + (dc + 1) * 128],
                                   pTall[mh].t[:, t0:t0 + tq], mh == 0, mh == 1, [vkey, pTall[mh].key], [pk])
                        act(ocr.t[:, 0:256].rearrange("p (a b) -> p a b", a=2)[:, :, t0:t0 + tq], pb[:, 0:2 * tq].rearrange("p (a b) -> p a b", a=2),
                            AF.Copy, [pk], [ocr.key])
                    yield
                for dc in range(2):
                    if not sample:
                        pb, pk = bank("M")
                        for mh in range(2):
                            mm(pb[:, 0:n], VT[0][:, mh, hd * 256 + dc * 128:hd * 256 + (dc + 1) * 128], pTall[mh].t[:, 0:n], mh == 0, mh == 1,
                               ["VT0", pTall[mh].key], [pk])
                        src_ps, src_k = pb[:, 0:n], pk
                    else:
                        src_ps, src_k = ocr.t[:, dc * 128:dc * 128 + n], ocr.key
                    tt("dve", oT[2][:, hd * 2 + dc, 0:n], src_ps, szc[dc].t[:, 0:n], ALU.mult, [src_k, szc[dc].key], ["oT2"])
                for t_ in (qd0, qd1, szc[0], szc[1], pTall[0], pTall[1]):
                    t_.free()
                if sample:
                    ocr.free()
                yield

            def L_stream():
                for j_ in range(8):
                    yield from lru_chunk(j_)
                for hd_ in range(4):
                    yield from attn_head(hd_)

            gens = [R_stream(0, 0), R_stream(1, 16), L_stream()]
            ratios = [1, 1, 1]
            while any(g is not None for g in gens):
                for gi_, g in enumerate(gens):
                    if g is None:
                        continue
                    for _ in range(ratios[gi_]):
                        try:
                            next(g)
                        except StopIteration:
                            gens[gi_] = None
                            break
            lorab.free()
            ALIAS_OK[0] = False
            assert len(freeFA) == NFA
            mlist[0] = MWIDE
            for kc in range(8):
                dma("sp", xT[:, kc, 0:n], d_xT[:, kc, tok0:tok0 + n], W=[XK[kc]])
            for m in range(8):
                wt, wk = wM.next()
                wvg = wview(wt, 384)
                wt2, wk2 = wM.next()
                wvb = wt2[:, 0:3 * 8 * 128].rearrange("p (b k c) -> p b k c", b=3, k=8)
                acc = fal()
                for br in range(3):
                    pb, pk = proj(wvg, br * 128, n, wk)
                    g = fal()
                    act(g.t[:, 0:n], pb[:, 0:n], AF.Sigmoid, [pk], [g.key])
                    pb2, pk2 = bank("M")
                    for kc in range(8):
                        mm(pb2[:, 0:n], wvb[:, br, kc, :], oT[br][:, kc, 0:n], kc == 0, kc == 7, [wk2, "oT%d" % br], [pk2])
                    if br == 0:
                        tt("dve", acc.t[:, 0:n], pb2[:, 0:n], g.t[:, 0:n], ALU.mult, [pk2, g.key], [acc.key])
                    else:
                        tt("dve", g.t[:, 0:n], pb2[:, 0:n], g.t[:, 0:n], ALU.mult, [pk2, g.key], [g.key])
                        if br == 1:
                            tt("pool", acc.t[:, 0:n], acc.t[:, 0:n], g.t[:, 0:n], ALU.add, [acc.key, g.key], [acc.key])
                        else:
                            tt("dve", mgT[:, m, 0:n], acc.t[:, 0:n], g.t[:, 0:n], ALU.add, [acc.key, g.key], ["mgT"])
                    g.free()
                acc.free()

            for oc in range(8):
                wt, wk = wM.next()
                wvo = wview(wt, 128)
                pb, pk = bank("M")
                for kc in range(8):
                    mm(pb[:, 0:n], wvo[:, kc, :], mgT[:, kc, 0:n], kc == 0, kc == 7, [wk, "mgT"], [pk])
                tt("dve", xT[:, oc, 0:n], pb[:, 0:n], xT[:, oc, 0:n], ALU.add, [pk, XK[oc]], [XK[oc]])
            mlist[0] = MNARROW
            rf = norm_rinv(xT, XK, n)
            for oc in range(8):
                stt(xT[:, oc, 0:n], xT[:, oc, 0:n], dcol("gf32", oc), rf.t[:, 0:n], ALU.mult, ALU.mult, [XK[oc], "dv", rf.key], [XK[oc]])
                dma("sp", o_yT[:, oc, tok0:tok0 + n], xT[:, oc, 0:n], R=[XK[oc]])
            rf.free()

        if DEBUG:
            for b_ in range(3):
                d_ = nc.dram_tensor("dbg_o%d" % b_, [128, 8, NMAX], BF16, kind="ExternalOutput").ap()
                dma("sp", d_, oT[b_][:], R=["oT%d" % b_])
            d_ = nc.dram_tensor("dbg_mg", [128, 8, NMAX], BF16, kind="ExternalOutput").ap()
            dma("sp", d_, mgT[:], R=["mgT"])
        dma("sp", o_rwp, st["p"]["RW"][:], R=["cRW_p"])
        dma("sp", o_shp, st["p"]["U"][:], R=["cU_p"])
        dma("sp", o_lrup, st["p"]["H"][:], R=["cH_p"])
        dma("sp", o_cvp, st["p"]["CV"][:], R=["cCV_p"])
        dma("sp", o_shs, st["s"]["U"][:], R=["cU_s"])
        dma("sp", o_lrus, st["s"]["H"][:], R=["cH_s"])
        dma("sp", o_cvs, st["s"]["CV"][:], R=["cCV_s"])
        for w_ in (wR, wL, wM):
            assert w_.ptr == len(w_.list), (w_.ptr, len(w_.list))
        P.emit(es)
    return nc


_NC_CACHE = {}


def kernel(x_prompt, x_sample, mem_prompt, state_rwkv, state_shift, state_lru, state_conv,
           cache_mem_k, cache_mem_v, g_norm, w_in, mu_shift, w0, w_decay, a0, w_aaa, k_k, k_a, r_k,
           gn_g, gn_b, conv_w, conv_b, w_rg_a, b_rg_a, w_rg_x, b_rg_x, lru_lambda, g_mem, w_mem_kv,
           w_br_a, w_br_b, w_br_c, w_out, g_final):
    f = lambda a: np.asarray(a, np.float32)
    x_prompt, x_sample, mem_prompt = f(x_prompt), f(x_sample), f(mem_prompt)
    perm = _colperm()
    w_in_l = np.ascontiguousarray(f(w_in)[0][:, perm].reshape(8, 128, NCOL).transpose(1, 0, 2))
    w_kv_l = np.ascontiguousarray(f(w_mem_kv)[0].reshape(8, 128, 2048).transpose(1, 0, 2))
    wbr = np.stack([f(w_br_a)[0], f(w_br_b)[0], f(w_br_c)[0]], 0)
    w_br_l = np.ascontiguousarray(wbr.reshape(3, 8, 128, 8, 128).transpose(2, 3, 0, 1, 4))
    w_out_l = np.ascontiguousarray(f(w_out)[0].reshape(8, 128, 8, 128).transpose(1, 2, 0, 3))
    wl = np.ascontiguousarray(np.concatenate([f(w_decay)[0], f(w_aaa)[0]], 0))
    mu = f(mu_shift)[0]
    cw = f(conv_w)[0]
    vecs = {"g_norm": f(g_norm)[0], "mu_r": mu[0:1024], "mu_k": mu[1024:2048], "mu_v": mu[2048:3072], "w0": f(w0)[0], "a0": f(a0)[0],
            "k_k": f(k_k)[0], "k_a": f(k_a)[0], "r_k": f(r_k)[0].reshape(1024), "gn_g": f(gn_g)[0], "gn_b": f(gn_b)[0],
            "cw0": cw[0], "cw1": cw[1], "cw2": cw[2], "cw3": cw[3], "conv_b": f(conv_b)[0], "b_rg_a": f(b_rg_a)[0], "b_rg_x": f(b_rg_x)[0],
            "lam": f(lru_lambda)[0], "g_mem": f(g_mem)[0], "g_final": f(g_final)}
    pvec = np.concatenate([_fm(vecs[nm]) for nm in PV_NAMES] + [mu[3072:3200].reshape(128, 1)], 1).astype(np.float32)
    pvec = np.ascontiguousarray(pvec)
    key = "nc"
    if key not in _NC_CACHE:
        _NC_CACHE[key] = build_nc()
    nc = _NC_CACHE[key]
    in_maps = []
    for c in range(NCORE):
        xs = x_sample[c * NSEQ:(c + 1) * NSEQ].reshape(TS, 1024)
        xa = np.concatenate([x_prompt[c], xs], 0)
        xT = np.ascontiguousarray(xa.T.reshape(8, 128, NTOK).transpose(1, 0, 2))
        memT = np.ascontiguousarray(mem_prompt[c].T.reshape(8, 128, 256).transpose(1, 0, 2))
        sl = slice(c * NSEQ, (c + 1) * NSEQ)
        ck = f(cache_mem_k)[0, sl]
        kT_s = np.ascontiguousarray(ck.transpose(0, 2, 3, 1).reshape(NSEQ, 4, 2, 128, 256).transpose(0, 3, 1, 2, 4).reshape(NSEQ, 128, 8, 256))
        cv = f(cache_mem_v)[0, sl].reshape(NSEQ, 2, 128, 1024)
        v_s = np.ascontiguousarray(cv.transpose(0, 2, 1, 3))
        srw = f(state_rwkv)[0, sl]
        st_rwkv = np.ascontiguousarray(srw.reshape(NSEQ, 8, 2, 64, 64).transpose(2, 4, 1, 0, 3).reshape(128, 8, NSEQ, 64))
        st_shift = np.ascontiguousarray(f(state_shift)[0, sl].reshape(NSEQ, 25, 128).transpose(2, 1, 0))
        st_lru = np.ascontiguousarray(f(state_lru)[0, sl].reshape(NSEQ, 8, 128).transpose(2, 1, 0))
        st_conv = np.ascontiguousarray(f(state_conv)[0, sl].reshape(NSEQ, 3, 8, 128).transpose(3, 2, 0, 1))
        in_maps.append({"xT": xT, "memT": memT, "w_in_l": w_in_l, "w_kv_l": w_kv_l, "w_br_l": w_br_l, "w_out_l": w_out_l, "wl": wl,
                        "w_rg_a": np.ascontiguousarray(f(w_rg_a)[0]), "w_rg_x": np.ascontiguousarray(f(w_rg_x)[0]), "pvec": pvec,
                        "kT_s": kT_s, "v_s": v_s, "st_rwkv": st_rwkv, "st_shift": st_shift, "st_lru": st_lru, "st_conv": st_conv})
    res = run_bass_kernel_spmd(nc, in_maps, core_ids=list(range(NCORE)))
    R = res.results
    if DEBUG:
        for k_ in ("dbg_o0", "dbg_o1", "dbg_o2", "dbg_mg"):
            DBG_OUT[k_] = np.asarray(R[0][k_]).astype(np.float32)
    y_p = np.empty((8, TP, 1024), np.float32); y_s = np.empty((128, SLEN, 1024), np.float32)
    rw_p = np.empty((1, 8, 16, 64, 64), np.float32); sh_p = np.empty((1, 8, 3200), np.float32)
    lru_p = np.empty((1, 8, 1024), np.float32); cv_p = np.empty((1, 8, 3, 1024), np.float32)
    mk_p = np.empty((1, 8, 256, 4, 256), np.float32); mv_p = np.empty((1, 8, 256, 4, 256), np.float32)
    rw_s = np.empty((1, 128, 16, 64, 64), np.float32); sh_s = np.empty((1, 128, 3200), np.float32)
    lru_s = np.empty((1, 128, 1024), np.float32); cv_s = np.empty((1, 128, 3, 1024), np.float32)

    def unrw(a):
        ns = a.shape[2]
        return a.reshape(2, 64, 8, ns, 64).transpose(3, 2, 0, 4, 1).reshape(ns, 16, 64, 64)
    for c in range(NCORE):
        r = R[c]
        y = r["yT"].transpose(2, 1, 0).reshape(NTOK, 1024)
        y_p[c] = y[:TP]
        sl = slice(c * NSEQ, (c + 1) * NSEQ)
        y_s[sl] = y[TP:].reshape(NSEQ, SLEN, 1024)
        rw_p[0, c] = unrw(r["rwkv_p"])[0]
        sh_p[0, c] = r["shift_p"].transpose(2, 1, 0).reshape(3200)
        lru_p[0, c] = r["lru_p"].transpose(2, 1, 0).reshape(1024)
        cv_p[0, c] = r["conv_p"].transpose(2, 3, 1, 0).reshape(3, 1024)
        mk_p[0, c] = r["memk_p"].reshape(128, 4, 2, 256).transpose(3, 1, 2, 0).reshape(256, 4, 256)
        mv_p[0, c] = r["memv_p"].transpose(1, 0, 2).reshape(256, 4, 256)
        rw_s[0, sl] = unrw(r["rwkv_s"])
        sh_s[0, sl] = r["shift_s"].transpose(2, 1, 0).reshape(NSEQ, 3200)
        lru_s[0, sl] = r["lru_s"].transpose(2, 1, 0).reshape(NSEQ, 1024)
        cv_s[0, sl] = r["conv_s"].transpose(2, 3, 1, 0).reshape(NSEQ, 3, 1024)
    return (y_p, y_s, rw_p, sh_p, lru_p, cv_p, mk_p, mv_p, rw_s, sh_s, lru_s, cv_s)
```
